# Optimizing a Trainium2 kernel written in Bass

```python
import math
import jax, jax.numpy as jnp
from jax import lax
import numpy as np

D_MODEL = 2048
BATCH = 1
SEQ = 8192
DEPTH = 4

N_MIXERS = 2
HEAD_DIM = 128
GDN_QK_HEADS = D_MODEL // HEAD_DIM
GDN_V_HEADS = 2 * GDN_QK_HEADS
GDN_CONV = 4
GDN_CHUNK = 64
GDN_DQK = GDN_QK_HEADS * HEAD_DIM
GDN_DV = GDN_V_HEADS * HEAD_DIM
GDN_CONV_CH = 2 * GDN_DQK + GDN_DV
GDN_IN = GDN_CONV_CH + GDN_DV + 2 * GDN_V_HEADS
NSA_Q_HEADS = D_MODEL // HEAD_DIM
NSA_KV_HEADS = NSA_Q_HEADS // 4
NSA_GROUP = NSA_Q_HEADS // NSA_KV_HEADS
N_BRANCH = 3
CMP_LEN = 32
CMP_STRIDE = 16
SLC_LEN = 64
SLC_TOP = 16
WINDOW = 512
NSA_QBLOCK = 64
NSA_DQ = NSA_Q_HEADS * HEAD_DIM
NSA_DKV = NSA_KV_HEADS * HEAD_DIM
NSA_IN = NSA_DQ + 6 * NSA_DKV + NSA_Q_HEADS * N_BRANCH
_FF_RAW = -(-8 * D_MODEL // 3)
D_FF = -(-_FF_RAW // 256) * 256
ROPE_THETA = 10000.0
POS_OFFSET_MAX = 32768
LN_EPS = 1e-5
NORM_EPS = 1e-6
ALPHA_DN = (2 * DEPTH) ** 0.25
BETA_DN = (8 * DEPTH) ** -0.25

kernel_name = 'hybrid_gdn_nsa_deepnorm_adaln'


def layer_norm(x, g, b):
    xf = x.astype(jnp.float32)
    mu = jnp.mean(xf, -1, keepdims=True)
    var = jnp.mean(jnp.square(xf - mu), -1, keepdims=True)
    return ((xf - mu) * lax.rsqrt(var + LN_EPS)).astype(x.dtype) * g + b


def l2norm(t):
    tf = t.astype(jnp.float32)
    return tf * lax.rsqrt(jnp.sum(tf * tf, -1, keepdims=True) + NORM_EPS)


def rope_tables(positions):
    inv = ROPE_THETA ** (-jnp.arange(0, HEAD_DIM, 2, dtype=jnp.float32) / HEAD_DIM)
    ang = positions.astype(jnp.float32)[..., None] * inv
    return jnp.cos(ang)[:, :, None, :], jnp.sin(ang)[:, :, None, :]


def apply_rope(t, cos, sin):
    t1, t2 = jnp.split(t.astype(jnp.float32), 2, axis=-1)
    return jnp.concatenate([t1 * cos - t2 * sin, t2 * cos + t1 * sin], -1).astype(t.dtype)


def masked_softmax(s, mask):
    s = jnp.where(mask, s.astype(jnp.float32), -jnp.inf)
    m = jnp.max(s, -1, keepdims=True)
    m = jnp.where(jnp.isfinite(m), m, 0.0)
    p = jnp.exp(s - m)
    den = jnp.sum(p, -1, keepdims=True)
    return p / jnp.where(den > 0, den, 1.0)


def causal_depthwise_conv(t, w):
    return lax.conv_general_dilated(t, w[:, None, :].astype(t.dtype), window_strides=(1,),
                                    padding=[(GDN_CONV - 1, 0)],
                                    dimension_numbers=('NWC', 'WIO', 'NWC'),
                                    feature_group_count=t.shape[-1])


def gated_delta_rule(q, k, v, g, beta):
    B, S, H, dk = k.shape
    dv = v.shape[-1]
    C = GDN_CHUNK
    N = S // C

    def chunks(t):
        return jnp.moveaxis(t.astype(jnp.float32).reshape(B, N, C, H, *t.shape[3:]), 3, 1)

    q, k, v, g, beta = [chunks(t) for t in (q, k, v, g, beta)]
    gc = jnp.cumsum(g, -1)
    idx = jnp.arange(C)
    lower = idx[:, None] >= idx[None, :]
    strict = idx[:, None] > idx[None, :]
    diff = gc[..., :, None] - gc[..., None, :]
    decay = jnp.where(lower, jnp.exp(jnp.where(lower, diff, 0.0)), 0.0)
    kb = k * beta[..., None]
    L = jnp.where(strict, jnp.einsum('bhnid,bhnjd->bhnij', kb, k) * decay, 0.0)
    rhs = jnp.concatenate([v * beta[..., None], kb * jnp.exp(gc)[..., None]], -1)
    sol = lax.linalg.triangular_solve(L + jnp.eye(C, dtype=L.dtype), rhs, left_side=True,
                                      lower=True, unit_diagonal=True)
    u, w = sol[..., :dv], sol[..., dv:]
    a_qk = jnp.where(lower, jnp.einsum('bhnid,bhnjd->bhnij', q, k) * decay, 0.0)
    q_g = q * jnp.exp(gc)[..., None]
    g_last = gc[..., -1]
    k_g = k * jnp.exp(g_last[..., None] - gc)[..., None]
    xs = [jnp.moveaxis(t, 2, 0) for t in (q_g, k_g, u, w, a_qk, g_last)]

    def step(state, inp):
        qg, kg, uu, ww, aa, gl = inp
        v_new = uu - jnp.einsum('bhck,bhkv->bhcv', ww, state)
        o = jnp.einsum('bhck,bhkv->bhcv', qg, state) + jnp.einsum('bhij,bhjv->bhiv', aa, v_new)
        state = state * jnp.exp(gl)[..., None, None] + jnp.einsum('bhck,bhcv->bhkv', kg, v_new)
        return state, o

    state0 = jnp.zeros((B, H, dk, dv), jnp.float32)
    _, o = lax.scan(step, state0, xs)
    return jnp.transpose(o, (1, 0, 3, 2, 4)).reshape(B, S, H, dv)


def gated_deltanet(h, w_in, conv_w, a_log, dt_bias, norm_w, w_out):
    B, S, _ = h.shape
    proj = h @ w_in
    qkv, z, b, a = jnp.split(proj, [GDN_CONV_CH, GDN_CONV_CH + GDN_DV,
                                    GDN_CONV_CH + GDN_DV + GDN_V_HEADS], axis=-1)
    qkv = jax.nn.silu(causal_depthwise_conv(qkv, conv_w))
    q, k, v = jnp.split(qkv, [GDN_DQK, 2 * GDN_DQK], axis=-1)
    rep = GDN_V_HEADS // GDN_QK_HEADS
    q = jnp.repeat(l2norm(q.reshape(B, S, GDN_QK_HEADS, HEAD_DIM)) * HEAD_DIM ** -0.5, rep, axis=2)
    k = jnp.repeat(l2norm(k.reshape(B, S, GDN_QK_HEADS, HEAD_DIM)), rep, axis=2)
    v = v.reshape(B, S, GDN_V_HEADS, HEAD_DIM)
    beta = jax.nn.sigmoid(b.astype(jnp.float32))
    g = -jnp.exp(a_log.astype(jnp.float32)) * jax.nn.softplus(a.astype(jnp.float32) + dt_bias.astype(jnp.float32))
    o = gated_delta_rule(q, k, v, g, beta)
    o = o * lax.rsqrt(jnp.mean(o * o, -1, keepdims=True) + NORM_EPS)
    o = (o.astype(h.dtype) * norm_w) * jax.nn.silu(z.reshape(B, S, GDN_V_HEADS, HEAD_DIM))
    return o.reshape(B, S, GDN_DV) @ w_out


def nsa_attention(h, cos, sin, w_in, cmp_pe, cmp_w1, cmp_w2, w_out):
    B, S, _ = h.shape
    Hq, Hk, G, dh, QB = NSA_Q_HEADS, NSA_KV_HEADS, NSA_GROUP, HEAD_DIM, NSA_QBLOCK
    proj = h @ w_in
    q, kc, vc, ks, vs, kw, vw, gates = jnp.split(proj, [NSA_DQ + i * NSA_DKV for i in range(7)], axis=-1)
    q = apply_rope(q.reshape(B, S, Hq, dh), cos, sin).reshape(B, S, Hk, G, dh) * dh ** -0.5
    kc, ks, kw = [apply_rope(t.reshape(B, S, Hk, dh), cos, sin) for t in (kc, ks, kw)]
    vc, vs, vw = [t.reshape(B, S, Hk, dh) for t in (vc, vs, vw)]
    gates = jax.nn.sigmoid(gates.astype(jnp.float32)).reshape(B, S, Hk, G, N_BRANCH).astype(h.dtype)

    n_cmp = (S - CMP_LEN) // CMP_STRIDE + 1
    c_start = CMP_STRIDE * jnp.arange(n_cmp)
    cmp_idx = c_start[:, None] + jnp.arange(CMP_LEN)[None, :]
    cmp_end = c_start + CMP_LEN - 1

    def compress(t, pe, w1, w2):
        blk = t[:, cmp_idx] + pe[:, None, :]
        blk = jnp.moveaxis(blk, 3, 2).reshape(B, n_cmp, Hk, CMP_LEN * dh)
        return jax.nn.silu(blk @ w1) @ w2

    kcmp = compress(kc, cmp_pe[0], cmp_w1[0], cmp_w2[0])
    vcmp = compress(vc, cmp_pe[1], cmp_w1[1], cmp_w2[1])

    n_slc = S // SLC_LEN
    n_sel = min(SLC_TOP, n_slc)
    ks_blk = jnp.moveaxis(ks.reshape(B, n_slc, SLC_LEN, Hk, dh), 3, 1)
    vs_blk = jnp.moveaxis(vs.reshape(B, n_slc, SLC_LEN, Hk, dh), 3, 1)
    s_start = SLC_LEN * jnp.arange(n_slc)
    overlap = jnp.clip(jnp.minimum(c_start[:, None] + CMP_LEN, s_start[None, :] + SLC_LEN)
                       - jnp.maximum(c_start[:, None], s_start[None, :]), 0, None).astype(jnp.float32) / CMP_LEN
    b_ix = jnp.arange(B)[:, None, None, None]
    h_ix = jnp.arange(Hk)[None, None, :, None]

    kw_pad = jnp.pad(kw, ((0, 0), (WINDOW, 0), (0, 0), (0, 0)))
    vw_pad = jnp.pad(vw, ((0, 0), (WINDOW, 0), (0, 0), (0, 0)))

    def block(i):
        t0 = i * QB
        tpos = t0 + jnp.arange(QB)
        qb = lax.dynamic_slice_in_dim(q, t0, QB, 1)
        gb = lax.dynamic_slice_in_dim(gates, t0, QB, 1)
        s_c = jnp.einsum('bqhgd,bnhd->bqhgn', qb, kcmp)
        p_c = masked_softmax(s_c, (cmp_end[None, :] <= tpos[:, None])[None, :, None, None, :])
        o_c = jnp.einsum('bqhgn,bnhd->bqhgd', p_c.astype(vcmp.dtype), vcmp)
        imp = jnp.einsum('bqhgn,nj->bqhj', p_c, overlap)
        cur = (tpos // SLC_LEN)[None, :, None, None]
        j = jnp.arange(n_slc)
        forced = (j == 0) | (j == cur) | (j == cur - 1)
        imp = jnp.where(j <= cur, jnp.where(forced, jnp.inf, imp), -jnp.inf)
        _, sel = lax.top_k(imp, n_sel)
        k_sel = ks_blk[b_ix, h_ix, sel].reshape(B, QB, Hk, n_sel * SLC_LEN, dh)
        v_sel = vs_blk[b_ix, h_ix, sel].reshape(B, QB, Hk, n_sel * SLC_LEN, dh)
        key_pos = sel[..., None] * SLC_LEN + jnp.arange(SLC_LEN)
        m_s = (sel <= cur)[..., None] & (key_pos <= tpos[None, :, None, None, None])
        s_s = jnp.einsum('bqhgd,bqhmd->bqhgm', qb, k_sel)
        p_s = masked_softmax(s_s, m_s.reshape(B, QB, Hk, 1, n_sel * SLC_LEN))
        o_s = jnp.einsum('bqhgm,bqhmd->bqhgd', p_s.astype(v_sel.dtype), v_sel)
        kwb = lax.dynamic_slice_in_dim(kw_pad, t0, WINDOW + QB, 1)
        vwb = lax.dynamic_slice_in_dim(vw_pad, t0, WINDOW + QB, 1)
        kpos = t0 - WINDOW + jnp.arange(WINDOW + QB)
        m_w = (kpos[None, :] <= tpos[:, None]) & (kpos[None, :] > tpos[:, None] - WINDOW) & (kpos[None, :] >= 0)
        s_w = jnp.einsum('bqhgd,bkhd->bqhgk', qb, kwb)
        p_w = masked_softmax(s_w, m_w[None, :, None, None, :])
        o_w = jnp.einsum('bqhgk,bkhd->bqhgd', p_w.astype(vwb.dtype), vwb)
        return gb[..., 0:1] * o_c + gb[..., 1:2] * o_s + gb[..., 2:3] * o_w

    o = lax.map(block, jnp.arange(S // QB))
    o = jnp.moveaxis(o, 0, 1).reshape(B, S, NSA_DQ)
    return o @ w_out


def swiglu(h, w_gu, w_down):
    gate, up = jnp.split(h @ w_gu, 2, axis=-1)
    return (jax.nn.silu(gate) * up) @ w_down


def setup_inputs(seed: int = 0) -> dict:
    key = jax.random.key(seed)
    k = jax.random.split(key, 24)
    n_gdn = (DEPTH + N_MIXERS - 1) // N_MIXERS
    n_nsa = DEPTH // N_MIXERS

    def w(kk, shape, fan_in, gain=1.0):
        return jax.random.normal(kk, shape, jnp.float32) * (gain * fan_in ** -0.5)

    x = jax.random.normal(k[0], (BATCH, SEQ, D_MODEL), jnp.float32)
    c = jax.random.normal(k[1], (BATCH, D_MODEL), jnp.float32)
    start = jax.random.randint(k[2], (BATCH, 1), 0, POS_OFFSET_MAX, dtype=jnp.int32)
    positions = start + jnp.arange(SEQ, dtype=jnp.int32)[None, :]
    mod_w = w(k[3], (DEPTH, D_MODEL, 6 * D_MODEL), D_MODEL, 0.2)
    mod_b = 0.01 * jax.random.normal(k[4], (DEPTH, 6 * D_MODEL), jnp.float32)
    ln_g = 1.0 + 0.05 * jax.random.normal(k[5], (DEPTH, 2, D_MODEL), jnp.float32)
    ln_b = 0.01 * jax.random.normal(k[6], (DEPTH, 2, D_MODEL), jnp.float32)
    ffn_w_gu = w(k[7], (DEPTH, D_MODEL, 2 * D_FF), D_MODEL)
    ffn_w_down = w(k[8], (DEPTH, D_FF, D_MODEL), D_FF, BETA_DN)
    gdn_w_in = w(k[9], (n_gdn, D_MODEL, GDN_IN), D_MODEL)
    gdn_conv_w = w(k[10], (n_gdn, GDN_CONV, GDN_CONV_CH), GDN_CONV)
    gdn_a_log = jnp.log(jax.random.uniform(k[11], (n_gdn, GDN_V_HEADS), jnp.float32, 1.0, 16.0))
    dt = jnp.exp(jax.random.uniform(k[12], (n_gdn, GDN_V_HEADS), jnp.float32, math.log(1e-3), math.log(1e-1)))
    gdn_dt_bias = dt + jnp.log(-jnp.expm1(-dt))
    gdn_norm_w = 1.0 + 0.05 * jax.random.normal(k[13], (n_gdn, HEAD_DIM), jnp.float32)
    gdn_w_out = w(k[14], (n_gdn, GDN_DV, D_MODEL), GDN_DV, BETA_DN)
    nsa_w_in = w(k[15], (n_nsa, D_MODEL, NSA_IN), D_MODEL)
    nsa_cmp_pe = 0.1 * jax.random.normal(k[16], (n_nsa, 2, CMP_LEN, HEAD_DIM), jnp.float32)
    nsa_cmp_w1 = w(k[17], (n_nsa, 2, CMP_LEN * HEAD_DIM, HEAD_DIM), CMP_LEN * HEAD_DIM)
    nsa_cmp_w2 = w(k[18], (n_nsa, 2, HEAD_DIM, HEAD_DIM), HEAD_DIM)
    nsa_w_out = w(k[19], (n_nsa, NSA_DQ, D_MODEL), NSA_DQ, BETA_DN)
    return {'x': x, 'c': c, 'positions': positions, 'mod_w': mod_w, 'mod_b': mod_b,
            'ln_g': ln_g, 'ln_b': ln_b, 'ffn_w_gu': ffn_w_gu, 'ffn_w_down': ffn_w_down,
            'gdn_w_in': gdn_w_in, 'gdn_conv_w': gdn_conv_w, 'gdn_a_log': gdn_a_log,
            'gdn_dt_bias': gdn_dt_bias, 'gdn_norm_w': gdn_norm_w, 'gdn_w_out': gdn_w_out,
            'nsa_w_in': nsa_w_in, 'nsa_cmp_pe': nsa_cmp_pe, 'nsa_cmp_w1': nsa_cmp_w1,
            'nsa_cmp_w2': nsa_cmp_w2, 'nsa_w_out': nsa_w_out}


def reference(x, c, positions, mod_w, mod_b, ln_g, ln_b, ffn_w_gu, ffn_w_down,
              gdn_w_in, gdn_conv_w, gdn_a_log, gdn_dt_bias, gdn_norm_w, gdn_w_out,
              nsa_w_in, nsa_cmp_pe, nsa_cmp_w1, nsa_cmp_w2, nsa_w_out):
    cos, sin = rope_tables(positions)
    cond = jax.nn.silu(c)
    for i in range(DEPTH):
        mod = (cond @ mod_w[i] + mod_b[i])[:, None, :]
        sh1, sc1, ga1, sh2, sc2, ga2 = jnp.split(mod, 6, axis=-1)
        j = i // N_MIXERS
        h = x * (1 + sc1) + sh1
        if i % N_MIXERS == 0:
            y = gated_deltanet(h, gdn_w_in[j], gdn_conv_w[j], gdn_a_log[j], gdn_dt_bias[j],
                               gdn_norm_w[j], gdn_w_out[j])
        else:
            y = nsa_attention(h, cos, sin, nsa_w_in[j], nsa_cmp_pe[j], nsa_cmp_w1[j],
                              nsa_cmp_w2[j], nsa_w_out[j])
        x = layer_norm(ALPHA_DN * x + (1 + ga1) * y, ln_g[i, 0], ln_b[i, 0])
        h = x * (1 + sc2) + sh2
        y = swiglu(h, ffn_w_gu[i], ffn_w_down[i])
        x = layer_norm(ALPHA_DN * x + (1 + ga2) * y, ln_g[i, 1], ln_b[i, 1])
    return x
```

```python
import numpy as np
import concourse.bass as bass
import concourse.mybir as mybir
from concourse.bass_utils import run_bass_kernel_spmd
import ml_dtypes

F32 = mybir.dt.float32
BF16 = mybir.dt.bfloat16
I32 = mybir.dt.int32
AF = mybir.ActivationFunctionType
ALU = mybir.AluOpType
NPBF = ml_dtypes.bfloat16


class Buf:
    __slots__ = ("t", "name", "lw", "rd", "dsem", "dcnt", "excl")

    def __init__(self, t, name):
        self.t = t
        self.name = name
        self.lw = None
        self.rd = []
        self.dsem = None
        self.dcnt = 0
        self.excl = False

    def __getitem__(self, idx):
        return self.t[idx]


class Ctx:
    def __init__(self, nc):
        self.nc = nc
        self.eng = {"pe": nc.tensor, "dve": nc.vector, "act": nc.scalar, "pool": nc.gpsimd, "sp": nc.sync}
        self.sems = {}
        self.cnt = {}
        self.waited = {k: {} for k in self.eng}
        for k in self.eng:
            self.sems[k] = nc.semaphore("s_" + k).__enter__()
            self.cnt[k] = 0
        self.nbuf = 0
        self.ndsem = 0
        self.dram_out = []
        self.n_inst = 0
        self.n_wait = 0
        self._dma_bufs = []

    def sb(self, name, shape, dtype):
        t = self.nc.sbuf_tensor(name, list(shape), dtype).__enter__()
        return Buf(t, name)

    def ps(self, name, shape, dtype=F32):
        t = self.nc.psum_tensor(name, list(shape), dtype).__enter__()
        b = Buf(t, name)
        b.excl = True
        return b

    def dram(self, name, shape, dtype, kind):
        t = self.nc.dram_tensor(name, list(shape), dtype, kind=kind)
        b = Buf(t.ap(), name)
        if kind == "ExternalOutput":
            self.dram_out.append(b)
        return b

    def key(self, name):
        return Buf(None, name)

    def _need(self, reads, writes):
        need = {}
        for r in reads:
            if r.lw is not None:
                k, v = r.lw
                if need.get(k, 0) < v:
                    need[k] = v
        for w in writes:
            if w.lw is not None:
                k, v = w.lw
                if need.get(k, 0) < v:
                    need[k] = v
            for (k, v) in w.rd:
                if need.get(k, 0) < v:
                    need[k] = v
        return need

    def _emit_waits(self, e, need, skip_self=False):
        eng = self.eng[e]
        wd = self.waited[e]
        for k, v in need.items():
            if skip_self and k == e:
                continue
            if wd.get(k, 0) < v:
                sem = self.sems[k]
                eng.wait_ge(sem, v)
                wd[k] = v
                self.n_wait += 1

    def op(self, e, fn, reads=(), writes=(), skip_self=None):
        if skip_self is None:
            skip_self = (e == "pe")
        ex = [r for r in reads if r.excl]
        need = self._need(reads, list(writes) + ex)
        self._emit_waits(e, need, skip_self)
        inst = fn(self.eng[e])
        self.cnt[e] += 1
        inst.then_inc(self.sems[e], 1)
        tok = (e, self.cnt[e])
        for r in ex:
            r.lw = tok
            r.rd = []
        for r in reads:
            r.rd.append(tok)
        for w in writes:
            w.lw = tok
            w.rd = []
        self.n_inst += 1
        return inst

    def dma(self, q, out_ap, in_ap, reads=(), writes=(), **kw):
        need = self._need(reads, writes)
        self._emit_waits(q, need, skip_self=False)
        w = writes[0]
        if w.dsem is None:
            w.dsem = "d%d" % self.ndsem
            self.ndsem += 1
            self.sems[w.dsem] = self.nc.semaphore(w.dsem).__enter__()
            self._dma_bufs.append(w)
        inst = self.eng[q].dma_start(out=out_ap, in_=in_ap, **kw)
        w.dcnt += 16
        inst.then_inc(self.sems[w.dsem], 16)
        tok = (w.dsem, w.dcnt)
        for r in reads:
            r.rd.append(tok)
        for ww in writes:
            ww.lw = tok
            ww.rd = []
        self.n_inst += 1
        return inst

    def barrier(self):
        need = {}
        for k in self.eng:
            if self.cnt[k] > 0:
                need[k] = self.cnt[k]
        for k, sem in self.sems.items():
            if k.startswith("d"):
                pass
        for b in self._dma_bufs:
            if b.dcnt > 0:
                need[b.dsem] = b.dcnt
        for e in self.eng:
            self._emit_waits(e, dict(need))

    def finish(self, e="sp"):
        need = {}
        for b in self.dram_out:
            if b.lw is not None:
                k, v = b.lw
                need[k] = max(need.get(k, 0), v)
        for k in self.eng:
            if self.cnt[k] > 0:
                need[k] = max(need.get(k, 0), self.cnt[k])
        self._emit_waits(e, need)


ALPHA = 8.0 ** 0.25
LN_EPS = 1e-5
TL = 1024
NF = 44


def build_B(KC, final=False):
    nc = bass.Bass("TRN2", target_bir_lowering=False)
    c = Ctx(nc)
    oT = c.dram("oT", [128, KC * TL], BF16, "ExternalInput")
    xT = c.dram("xT", [16, 128, TL], F32, "ExternalInput")
    modv = c.dram("modv", [128, 128], F32, "ExternalInput")
    lnp = c.dram("lnp", [128, 64], F32, "ExternalInput")
    wo = c.dram("wo", [16, 128, KC * 128], F32, "ExternalInput")
    wgu = c.dram("wgu", [NF, 128, 16 * 256], F32, "ExternalInput")
    wd = c.dram("wd", [16, 128, NF * 128], F32, "ExternalInput")
    xo = c.dram("xo", [16, 128, TL], F32, "ExternalOutput")
    hn = c.dram("hn", [16, 128, TL], BF16, "ExternalOutput")
    zs_t = nc.dram_tensor("zs", [16, 128, TL], F32, kind="Internal").ap()
    xm_t = nc.dram_tensor("xm", [16, 128, TL], F32, kind="Internal").ap()
    zs = [Buf(zs_t[i], "zs%d" % i) for i in range(16)]
    xm = [Buf(xm_t[i], "xm%d" % i) for i in range(16)]
    xo_k = [Buf(xo.t[i], "xo%d" % i) for i in range(16)]
    hn_k = [Buf(hn.t[i], "hn%d" % i) for i in range(16)]
    c.dram_out = xo_k + hn_k

    R1 = c.sb("R1", [128, NF * TL], BF16)
    hT = c.sb("hT", [128, 16 * TL], BF16)
    wbuf = [c.sb("wb%d" % i, [128, 5632], BF16) for i in range(2)]
    TA = [c.sb("TA%d" % i, [128, TL], F32) for i in range(2)]
    TB = [c.sb("TB%d" % i, [128, TL], F32) for i in range(2)]
    TC = [c.sb("TC%d" % i, [128, TL], F32) for i in range(2)]
    HB = [c.sb("HB%d" % i, [128, TL], BF16) for i in range(2)]
    mean = c.sb("mean", [128, TL], F32)
    rstd = c.sb("rstd", [128, TL], F32)
    nmr = c.sb("nmr", [128, TL], F32)
    ones = c.sb("ones", [128, 128], F32)
    mv = c.sb("mv", [128, 128], F32)
    lp = c.sb("lp", [128, 64], F32)
    dv = c.sb("dv", [128, 64], F32)
    P = [c.ps("P%d" % i, [128, TL]) for i in range(4)]

    c.op("dve", lambda e: e.memset(ones[:], 1.0 / 2048.0), writes=[ones])
    c.dma("sp", mv[:], modv.t, writes=[mv])
    c.dma("sp", lp[:], lnp.t, writes=[lp])
    c.dma("sp", R1[:, 0:KC * TL], oT.t, writes=[R1])
    c.op("dve", lambda e: e.tensor_scalar(out=dv[:, 0:16], in0=mv[:, 32:48], scalar1=1.0, scalar2=1.0 / ALPHA, op0=ALU.add, op1=ALU.mult), reads=[mv], writes=[dv])
    c.op("dve", lambda e: e.tensor_scalar(out=dv[:, 16:32], in0=mv[:, 80:96], scalar1=1.0, scalar2=1.0 / ALPHA, op0=ALU.add, op1=ALU.mult), reads=[mv], writes=[dv])
    c.op("dve", lambda e: e.tensor_scalar(out=dv[:, 32:48], in0=mv[:, 64:80], scalar1=1.0, scalar2=None, op0=ALU.add), reads=[mv], writes=[dv])
    c.op("dve", lambda e: e.tensor_scalar(out=dv[:, 48:64], in0=mv[:, 112:128], scalar1=1.0, scalar2=None, op0=ALU.add), reads=[mv], writes=[dv])

    def ln_phase(nk, wsrc, rhs_buf, gcol, xsrc, second):
        for dc in range(16):
            wt = wbuf[dc % 2]
            c.dma("pool", wt[:, 0:nk * 128], wsrc.t[dc], writes=[wt])
            yp = P[dc % 2]
            for half in range(2):
                for k in range(nk):
                    c.op("pe", lambda e: e.matmul(yp[:, half * 512:(half + 1) * 512], wt[:, k * 128:(k + 1) * 128],
                                                  rhs_buf[:, k * TL + half * 512: k * TL + (half + 1) * 512],
                                                  start=(k == 0), stop=(k == nk - 1)),
                         reads=[wt, rhs_buf], writes=[yp])
            xt = TA[dc % 2]
            c.dma("sp", xt[:], xsrc[dc].t, reads=[xsrc[dc]], writes=[xt])
            z = TB[dc % 2]
            c.op("dve", lambda e: e.scalar_tensor_tensor(out=z[:], in0=yp[:], scalar=dv[:, gcol + dc:gcol + dc + 1], in1=xt[:],
                                                         op0=ALU.mult, op1=ALU.add), reads=[yp, xt, dv], writes=[z])
            zq = TC[dc % 2]
            c.op("act", lambda e: e.activation(out=zq[:], in_=z[:], func=AF.Square), reads=[z], writes=[zq])
            for half in range(2):
                sl = slice(half * 512, (half + 1) * 512)
                c.op("pe", lambda e: e.matmul(P[2][:, sl], ones[:], z[:, sl], start=(dc == 0), stop=(dc == 15)),
                     reads=[ones, z], writes=[P[2]])
                c.op("pe", lambda e: e.matmul(P[3][:, sl], ones[:], zq[:, sl], start=(dc == 0), stop=(dc == 15)),
                     reads=[ones, zq], writes=[P[3]])
            c.dma("act", zs[dc].t, z[:], reads=[z], writes=[zs[dc]])
        c.op("act", lambda e: e.activation(out=mean[:], in_=P[2][:], func=AF.Copy), reads=[P[2]], writes=[mean])
        m2 = TA[0]
        c.op("dve", lambda e: e.tensor_tensor(out=m2[:], in0=mean[:], in1=mean[:], op=ALU.mult), reads=[mean], writes=[m2])
        var = TB[0]
        c.op("dve", lambda e: e.tensor_tensor(out=var[:], in0=P[3][:], in1=m2[:], op=ALU.subtract), reads=[P[3], m2], writes=[var])
        sd = TC[0]
        c.op("dve", lambda e: e.tensor_scalar(out=var[:], in0=var[:], scalar1=LN_EPS / ALPHA ** 2, scalar2=None, op0=ALU.add), reads=[var], writes=[var])
        c.op("act", lambda e: e.activation(out=sd[:], in_=var[:], func=AF.Sqrt), reads=[var], writes=[sd])
        c.op("dve", lambda e: e.reciprocal(out=rstd[:], in_=sd[:]), reads=[sd], writes=[rstd])
        c.op("dve", lambda e: e.scalar_tensor_tensor(out=nmr[:], in0=mean[:], scalar=-1.0, in1=rstd[:], op0=ALU.mult, op1=ALU.mult),
             reads=[mean, rstd], writes=[nmr])
        for dc in range(16):
            zt = TA[dc % 2]
            c.dma("sp", zt[:], zs[dc].t, reads=[zs[dc]], writes=[zt])
            t1 = TB[dc % 2]
            c.op("dve", lambda e: e.tensor_tensor(out=t1[:], in0=zt[:], in1=rstd[:], op=ALU.mult), reads=[zt, rstd], writes=[t1])
            c.op("dve", lambda e: e.tensor_tensor(out=t1[:], in0=t1[:], in1=nmr[:], op=ALU.add), reads=[t1, nmr], writes=[t1])
            xn = TC[dc % 2]
            if not second:
                c.op("act", lambda e: e.activation(out=xn[:], in_=t1[:], func=AF.Identity, scale=lp[:, dc:dc + 1], bias=lp[:, 16 + dc:17 + dc]),
                     reads=[t1, lp], writes=[xn])
                c.op("act", lambda e: e.activation(out=hT[:, dc * TL:(dc + 1) * TL], in_=xn[:], func=AF.Identity, scale=dv[:, 32 + dc:33 + dc],
                                                   bias=mv[:, 48 + dc:49 + dc]), reads=[xn, dv, mv], writes=[hT])
                c.dma("act", xm[dc].t, xn[:], reads=[xn], writes=[xm[dc]])
            else:
                c.op("act", lambda e: e.activation(out=xn[:], in_=t1[:], func=AF.Identity, scale=lp[:, 32 + dc:33 + dc], bias=lp[:, 48 + dc:49 + dc]),
                     reads=[t1, lp], writes=[xn])
                c.dma("act", xo_k[dc].t, xn[:], reads=[xn], writes=[xo_k[dc]])
                hb = HB[dc % 2]
                c.op("act", lambda e: e.activation(out=hb[:], in_=xn[:], func=AF.Identity, scale=dv[:, 48 + dc:49 + dc],
                                                   bias=mv[:, 96 + dc:97 + dc]), reads=[xn, dv, mv], writes=[hb])
                c.dma("act", hn_k[dc].t, hb[:], reads=[hb], writes=[hn_k[dc]])

    xin = [Buf(xT.t[i], "xin%d" % i) for i in range(16)]
    ln_phase(KC, wo, R1, 0, xin, False)

    for f in range(NF):
        wt = wbuf[f % 2]
        c.dma("pool", wt[:, 0:4096], wgu.t[f], writes=[wt])
        gp = P[f % 2]
        up = P[2 + f % 2]
        for (pp, off) in ((gp, 0), (up, 128)):
            for half in range(2):
                for dc in range(16):
                    c.op("pe", lambda e: e.matmul(pp[:, half * 512:(half + 1) * 512], wt[:, dc * 256 + off: dc * 256 + off + 128],
                                                  hT[:, dc * TL + half * 512: dc * TL + (half + 1) * 512],
                                                  start=(dc == 0), stop=(dc == 15)), reads=[wt, hT], writes=[pp])
        sg = TA[f % 2]
        c.op("act", lambda e: e.activation(out=sg[:], in_=gp[:], func=AF.Silu), reads=[gp], writes=[sg])
        c.op("dve", lambda e: e.tensor_tensor(out=R1[:, f * TL:(f + 1) * TL], in0=sg[:], in1=up[:], op=ALU.mult), reads=[sg, up], writes=[R1])

    ln_phase(NF, wd, R1, 16, xm, True)
    c.finish()
    return nc


def prep_B_weights(w_out, w_gu, w_down):
    Din = w_out.shape[0]
    KC = Din // 128
    wo = np.ascontiguousarray(w_out.reshape(KC, 128, 16, 128).transpose(2, 1, 0, 3)).reshape(16, 128, KC * 128)
    g = w_gu[:, :5632].reshape(16, 128, NF, 128)
    u = w_gu[:, 5632:].reshape(16, 128, NF, 128)
    gu = np.stack([g, u], axis=3)
    wgu = np.ascontiguousarray(gu.transpose(2, 1, 0, 3, 4)).reshape(NF, 128, 16 * 256)
    wd = np.ascontiguousarray(w_down.reshape(NF, 128, 16, 128).transpose(2, 1, 0, 3)).reshape(16, 128, NF * 128)
    return wo, wgu, wd


def fm(v):
    return np.ascontiguousarray(v.reshape(16, 128).T)


NORM_EPS = 1e-6


def gdn_consts():
    i = np.arange(128)
    U = (i[:, None] <= i[None, :]).astype(np.float32)
    ML = (i[:, None] > i[None, :]).astype(np.float32)
    ON = np.ones((128, 128), np.float32)
    ID = np.eye(128, dtype=np.float32)
    return np.concatenate([U] * 4 + [ML] * 4 + [ID] * 4 + [ON, -ON], axis=1)


def build_G(NT=16, stage=99):
    nc = bass.Bass("TRN2", target_bir_lowering=False)
    c = Ctx(nc)
    hT = c.dram("hT", [NT, 128, 16 * 512], BF16, "ExternalInput")
    wf = c.dram("wf", [128, 16 * 1024], F32, "ExternalInput")
    wt = c.dram("wt", [128, 16 * 520], F32, "ExternalInput")
    cw = c.dram("cw", [128, 32], F32, "ExternalInput")
    hp = c.dram("hp", [128, 8], F32, "ExternalInput")
    nw = c.dram("nw", [128, 512], F32, "ExternalInput")
    cst = c.dram("cst", [128, 1792], F32, "ExternalInput")
    oo = c.dram("oo", [NT * 4, 128, 512], BF16, "ExternalOutput")
    oo_k = [Buf(oo.t[i], "oo%d" % i) for i in range(NT * 4)]
    c.dram_out = oo_k

    Wf = c.sb("Wf", [128, 16 * 1024], BF16)
    Wt = c.sb("Wt", [128, 16 * 520], BF16)
    CW = c.sb("CW", [128, 32], F32)
    HP = c.sb("HP", [128, 8], F32)
    NEA = c.sb("NEA", [128, 4], F32)
    NW = c.sb("NW", [128, 512], F32)
    CS = c.sb("CS", [128, 1792], F32)
    IDb = c.sb("IDb", [128, 128], BF16)
    U4 = CS[:, 0:512]; ML4 = CS[:, 512:1024]; ID4 = CS[:, 1024:1536]
    U = CS[:, 0:128]; ID = CS[:, 1024:1152]; ON = CS[:, 1536:1664]; NG = CS[:, 1664:1792]
    HT = [c.sb("HT%d" % i, [128, 16 * 512], BF16) for i in range(2)]
    XC = [c.sb("XC_%d" % i, [128, 515], F32) for i in range(8)]
    ACC = [c.sb("ACC%d" % i, [128, 512], F32) for i in range(2)]
    QS = [c.sb("QS%d" % i, [128, 512], F32) for i in range(4)]
    SQ = c.sb("SQ0", [128, 512], F32)
    RN = c.sb("RN0", [128, 512], F32)
    QT = [c.sb("QT_%d" % i, [128, 512], BF16) for i in range(2)]
    KT = [c.sb("KT_%d" % i, [128, 512], BF16) for i in range(2)]
    VS = [c.sb("VS%d" % i, [128, 512], BF16) for i in range(4)]
    KTOK = c.sb("KTOK", [128, 1024], BF16)
    VTOK = c.sb("VTOK", [128, 2048], BF16)
    SZ = [c.sb("SZ_%d" % ch, [128, 512], BF16) for ch in range(4)]

    def sm(name):
        return [c.sb("%s_%d" % (name, ch), [128, 4], F32) for ch in range(4)]
    BETA = sm("BETA"); G = sm("G"); EGC = sm("EGC"); EKG = sm("EKG"); EGL = sm("EGL"); BEG = sm("BEG"); GCs = sm("GCs"); TM4 = sm("TM4")

    def big(name, dt=F32):
        return c.sb(name, [128, 512], dt)
    GU4 = big("GU4"); E1 = big("E1"); E2 = big("E2"); DL4 = big("DL4"); DTI4 = big("DTI4")
    L4 = big("L4"); LT4 = big("LT4"); PA4 = big("PA4"); PTA4 = big("PTA4"); PB4 = big("PB4"); PTB4 = big("PTB4"); RA4 = big("RA4"); RB4 = big("RB4")
    TT4 = big("TT4", BF16); VB4 = big("VB4", BF16); KBG4 = big("KBG4", BF16); KG4 = big("KG4", BF16); AQT4 = big("AQT4", BF16)
    NWT4 = big("NWT4", BF16); VN4 = big("VN4", BF16); AVs4 = big("AVs4"); O4 = big("O4"); OSQ = big("OSQ"); OG4 = big("OG4")
    St4 = big("St4"); Sb4 = big("Sb4", BF16)
    SSQ = c.sb("SSQ", [128, 4], F32)
    RINV = c.sb("RINV", [128, 4], F32)
    OUT = [c.sb("OUT%d" % b, [128, 512], BF16) for b in range(2)]

    B = [c.ps("B%d" % i, [128, 512]) for i in range(8)]
    TP4 = B[4].t[:, :].bitcast(BF16)

    def hs(h):
        return slice(h * 128, (h + 1) * 128)

    c.dma("sp", CS[:], cst.t, writes=[CS])
    c.dma("sp", CW[:], cw.t, writes=[CW])
    c.dma("sp", HP[:], hp.t, writes=[HP])
    c.dma("sp", NW[:], nw.t, writes=[NW])
    c.dma("pool", Wf[:], wf.t, writes=[Wf])
    c.dma("pool", Wt[:], wt.t, writes=[Wt])
    c.op("dve", lambda e: e.tensor_copy(out=IDb[:], in_=ID), reads=[CS], writes=[IDb])
    c.op("act", lambda e: e.activation(out=NEA[:], in_=HP[:, 0:4], func=AF.Exp), reads=[HP], writes=[NEA])
    c.op("dve", lambda e: e.tensor_scalar(out=NEA[:], in0=NEA[:], scalar1=-1.0, scalar2=None, op0=ALU.mult), reads=[NEA], writes=[NEA])
    c.op("dve", lambda e: e.memset(St4[:], 0.0), writes=[St4])
    c.op("dve", lambda e: e.memset(Sb4[:], 0.0), writes=[Sb4])
    for cc in range(8):
        c.op("dve", lambda e: e.memset(XC[cc][:, 0:3], 0.0), writes=[XC[cc]])

    def acopy(out_buf, out_ap, in_buf, in_ap, scale=None):
        if scale is None:
            c.op("act", lambda e: e.activation(out=out_ap, in_=in_ap, func=AF.Copy), reads=[in_buf], writes=[out_buf])
        else:
            c.op("act", lambda e: e.activation(out=out_ap, in_=in_ap, func=AF.Copy, scale=scale), reads=[in_buf], writes=[out_buf])

    def dcopy(out_buf, out_ap, in_buf, in_ap):
        c.op("dve", lambda e: e.tensor_copy(out=out_ap, in_=in_ap), reads=[in_buf], writes=[out_buf])


    if stage == 0:
        c.finish()
        return nc
    for tt in range(NT):
        ht = HT[tt % 2]
        c.dma("sp", ht[:], hT.t[tt], writes=[ht])
        for cc in range(8):
            pf = B[cc % 2]
            for dc in range(16):
                c.op("pe", lambda e: e.matmul(pf[:], Wf[:, dc * 1024 + cc * 128: dc * 1024 + (cc + 1) * 128], ht[:, dc * 512:(dc + 1) * 512],
                                              start=(dc == 0), stop=(dc == 15)), reads=[Wf, ht], writes=[pf])
            xc = XC[cc]
            acopy(xc, xc[:, 3:515], pf, pf[:])
            acc = ACC[cc % 2]
            c.op("dve", lambda e: e.tensor_scalar(out=acc[:], in0=xc[:, 0:512], scalar1=CW[:, cc * 4:cc * 4 + 1], scalar2=None, op0=ALU.mult),
                 reads=[xc, CW], writes=[acc])
            for j in range(1, 4):
                c.op("dve", lambda e: e.scalar_tensor_tensor(out=acc[:], in0=xc[:, j:j + 512], scalar=CW[:, cc * 4 + j:cc * 4 + j + 1], in1=acc[:],
                                                             op0=ALU.mult, op1=ALU.add), reads=[xc, CW, acc], writes=[acc])
            c.op("pool", lambda e: e.tensor_copy(out=xc[:, 0:3], in_=xc[:, 512:515]), reads=[xc], writes=[xc])
            if cc < 4:
                c.op("act", lambda e: e.activation(out=QS[cc][:], in_=acc[:], func=AF.Silu), reads=[acc], writes=[QS[cc]])
            else:
                c.op("act", lambda e: e.activation(out=VS[cc - 4][:], in_=acc[:], func=AF.Silu), reads=[acc], writes=[VS[cc - 4]])
        if stage == 1:
            continue
        for qi in range(4):
            c.op("act", lambda e: e.activation(out=SQ[:], in_=QS[qi][:], func=AF.Square), reads=[QS[qi]], writes=[SQ])
            c.op("pe", lambda e: e.matmul(B[3][:], ON, SQ[:], start=True, stop=True), reads=[CS, SQ], writes=[B[3]])
            c.op("dve", lambda e: e.tensor_scalar(out=RN[:], in0=B[3][:], scalar1=NORM_EPS, scalar2=None, op0=ALU.add), reads=[B[3]], writes=[RN])
            c.op("act", lambda e: e.activation(out=RN[:], in_=RN[:], func=AF.Sqrt), reads=[RN], writes=[RN])
            c.op("dve", lambda e: e.reciprocal(out=RN[:], in_=RN[:]), reads=[RN], writes=[RN])
            dst = QT[qi] if qi < 2 else KT[qi - 2]
            sc = (128.0 ** -0.5) if qi < 2 else 1.0
            c.op("dve", lambda e: e.scalar_tensor_tensor(out=dst[:], in0=QS[qi][:], scalar=sc, in1=RN[:], op0=ALU.mult, op1=ALU.mult),
                 reads=[QS[qi], RN], writes=[dst])
        if stage == 2:
            continue
        TPv = TP4
        for ch in range(4):
            for k in range(2):
                o_ = (ch * 2 + k) * 128
                c.op("pe", lambda e: e.transpose(TPv[:, o_:o_ + 128], KT[k][:, ch * 128:(ch + 1) * 128], IDb[:]), reads=[KT[k], IDb], writes=[B[4]])
        acopy(KTOK, KTOK[:], B[4], TPv[:, 0:1024])
        for half in range(2):
            for chh in range(2):
                ch = half * 2 + chh
                for h in range(4):
                    o_ = (chh * 4 + h) * 128
                    c.op("pe", lambda e: e.transpose(TPv[:, o_:o_ + 128], VS[h][:, ch * 128:(ch + 1) * 128], IDb[:]), reads=[VS[h], IDb], writes=[B[4]])
            dcopy(VTOK, VTOK[:, half * 1024:(half + 1) * 1024], B[4], TPv[:, 0:1024])
        if stage == 3:
            continue
        for ch in range(4):
            for dc in range(16):
                c.op("pe", lambda e: e.matmul(B[2][:], ht[:, dc * 512 + ch * 128: dc * 512 + (ch + 1) * 128], Wt[:, dc * 520: dc * 520 + 512],
                                              start=(dc == 0), stop=(dc == 15)), reads=[ht, Wt], writes=[B[2]])
            c.op("act", lambda e: e.activation(out=SZ[ch][:], in_=B[2][:], func=AF.Silu), reads=[B[2]], writes=[SZ[ch]])
            for dc in range(16):
                c.op("pe", lambda e: e.matmul(B[3][:, 0:8], ht[:, dc * 512 + ch * 128: dc * 512 + (ch + 1) * 128], Wt[:, dc * 520 + 512: dc * 520 + 520],
                                              start=(dc == 0), stop=(dc == 15)), reads=[ht, Wt], writes=[B[3]])
            be = BETA[ch]; g = G[ch]; t4 = TM4[ch]
            c.op("act", lambda e: e.activation(out=be[:], in_=B[3][:, 0:4], func=AF.Sigmoid), reads=[B[3]], writes=[be])
            c.op("dve", lambda e: e.tensor_tensor(out=t4[:], in0=B[3][:, 4:8], in1=HP[:, 4:8], op=ALU.add), reads=[B[3], HP], writes=[t4])
            c.op("act", lambda e: e.activation(out=t4[:], in_=t4[:], func=AF.Exp), reads=[t4], writes=[t4])
            c.op("dve", lambda e: e.tensor_scalar(out=t4[:], in0=t4[:], scalar1=1.0, scalar2=None, op0=ALU.add), reads=[t4], writes=[t4])
            c.op("act", lambda e: e.activation(out=t4[:], in_=t4[:], func=AF.Ln), reads=[t4], writes=[t4])
            c.op("dve", lambda e: e.tensor_tensor(out=g[:], in0=t4[:], in1=NEA[:], op=ALU.mult), reads=[t4, NEA], writes=[g])
            c.op("pe", lambda e: e.matmul(B[3][:, 8:12], U, g[:], start=True, stop=True), reads=[CS, g], writes=[B[3]])
            c.op("pe", lambda e: e.matmul(B[3][:, 12:16], ON, g[:], start=True, stop=True), reads=[CS, g], writes=[B[3]])
            c.op("act", lambda e: e.activation(out=EGC[ch][:], in_=B[3][:, 8:12], func=AF.Exp), reads=[B[3]], writes=[EGC[ch]])
            c.op("act", lambda e: e.activation(out=EGL[ch][:], in_=B[3][:, 12:16], func=AF.Exp), reads=[B[3]], writes=[EGL[ch]])
            c.op("act", lambda e: e.activation(out=GCs[ch][:], in_=B[3][:, 8:12], func=AF.Copy), reads=[B[3]], writes=[GCs[ch]])
            c.op("dve", lambda e: e.tensor_tensor(out=t4[:], in0=B[3][:, 12:16], in1=GCs[ch][:], op=ALU.subtract), reads=[B[3], GCs[ch]], writes=[t4])
            c.op("act", lambda e: e.activation(out=EKG[ch][:], in_=t4[:], func=AF.Exp), reads=[t4], writes=[EKG[ch]])
            c.op("dve", lambda e: e.tensor_tensor(out=BEG[ch][:], in0=be[:], in1=EGC[ch][:], op=ALU.mult), reads=[be, EGC[ch]], writes=[BEG[ch]])
        if stage == 4:
            continue
        for ch in range(4):
            csl = slice(ch * 128, (ch + 1) * 128)
            gch = G[ch]; be = BETA[ch]
            for h in range(4):
                c.op("dve", lambda e: e.tensor_scalar(out=GU4[:, hs(h)], in0=U, scalar1=gch[:, h:h + 1], scalar2=None, op0=ALU.mult), reads=[CS, gch], writes=[GU4])
            for h in range(4):
                c.op("pe", lambda e: e.matmul(B[5][:, hs(h)], GU4[:, hs(h)], ON, start=True, stop=False), reads=[GU4, CS], writes=[B[5]])
                c.op("pe", lambda e: e.matmul(B[5][:, hs(h)], NG, GU4[:, hs(h)], start=False, stop=True), reads=[GU4, CS], writes=[B[5]])
            for k in range(2):
                c.op("pe", lambda e: e.matmul(B[6][:, hs(k)], KT[k][:, csl], KT[k][:, csl], start=True, stop=True), reads=[KT[k]], writes=[B[6]])
                c.op("pe", lambda e: e.matmul(B[6][:, hs(2 + k)], KT[k][:, csl], QT[k][:, csl], start=True, stop=True), reads=[KT[k], QT[k]], writes=[B[6]])
            if stage == 41:
                continue
            c.op("act", lambda e: e.activation(out=E1[:], in_=B[5][:], func=AF.Exp), reads=[B[5]], writes=[E1])
            c.op("act", lambda e: e.activation(out=E2[:], in_=B[5][:], func=AF.Exp, scale=-1.0), reads=[B[5]], writes=[E2])
            c.op("dve", lambda e: e.scalar_tensor_tensor(out=DL4[:], in0=E1[:], scalar=1.0, in1=ML4, op0=ALU.min, op1=ALU.mult), reads=[E1, CS], writes=[DL4])
            c.op("dve", lambda e: e.scalar_tensor_tensor(out=DTI4[:], in0=E2[:], scalar=1.0, in1=U4, op0=ALU.min, op1=ALU.mult), reads=[E2, CS], writes=[DTI4])
            if stage == 42:
                continue
            for h in range(4):
                k = h // 2
                c.op("dve", lambda e: e.scalar_tensor_tensor(out=L4[:, hs(h)], in0=B[6][:, hs(k)], scalar=be[:, h:h + 1], in1=DL4[:, hs(h)], op0=ALU.mult, op1=ALU.mult),
                     reads=[B[6], be, DL4], writes=[L4])
                c.op("dve", lambda e: e.tensor_tensor(out=AQT4[:, hs(h)], in0=B[6][:, hs(2 + k)], in1=DTI4[:, hs(h)], op=ALU.mult), reads=[B[6], DTI4], writes=[AQT4])
            if stage == 43:
                continue
            for h in range(4):
                c.op("pe", lambda e: e.matmul(B[7][:, hs(h)], L4[:, hs(h)], ID, start=True, stop=True), reads=[L4, CS], writes=[B[7]])
            acopy(LT4, LT4[:], B[7], B[7][:])
            c.op("dve", lambda e: e.scalar_tensor_tensor(out=RA4[:], in0=B[7][:], scalar=-1.0, in1=ID4, op0=ALU.mult, op1=ALU.add), reads=[B[7], CS], writes=[RA4])
            if stage == 44:
                continue
            for h in range(4):
                k = h // 2
                vo = (ch * 4 + h) * 128
                ko = (ch * 2 + k) * 128
                c.op("pool", lambda e: e.tensor_scalar(out=VB4[:, hs(h)], in0=VTOK[:, vo:vo + 128], scalar1=be[:, h:h + 1], scalar2=None, op0=ALU.mult),
                     reads=[VTOK, be], writes=[VB4])
                c.op("pool", lambda e: e.tensor_scalar(out=KBG4[:, hs(h)], in0=KTOK[:, ko:ko + 128], scalar1=BEG[ch][:, h:h + 1], scalar2=None, op0=ALU.mult),
                     reads=[KTOK, BEG[ch]], writes=[KBG4])
                c.op("pool", lambda e: e.tensor_scalar(out=KG4[:, hs(h)], in0=KTOK[:, ko:ko + 128], scalar1=EKG[ch][:, h:h + 1], scalar2=None, op0=ALU.mult),
                     reads=[KTOK, EKG[ch]], writes=[KG4])
            if stage == 5:
                continue
            Pc, PTc, Rc = L4, LT4, RA4
            for lvl in range(1, 7):
                Pn = PA4 if lvl % 2 == 1 else PB4
                PTn = PTA4 if lvl % 2 == 1 else PTB4
                Rn = RB4 if lvl % 2 == 1 else RA4
                if lvl == 6:
                    Rn = TT4
                for h in range(4):
                    c.op("pe", lambda e: e.matmul(B[5][:, hs(h)], PTc[:, hs(h)], Pc[:, hs(h)], start=True, stop=True), reads=[PTc, Pc], writes=[B[5]])
                acopy(Pn, Pn[:], B[5], B[5][:])
                if lvl < 6:
                    for h in range(4):
                        c.op("pe", lambda e: e.matmul(B[6][:, hs(h)], Pc[:, hs(h)], PTc[:, hs(h)], start=True, stop=True), reads=[PTc, Pc], writes=[B[6]])
                    dcopy(PTn, PTn[:], B[6], B[6][:])
                for h in range(4):
                    c.op("pe", lambda e: e.matmul(B[7][:, hs(h)], Pn[:, hs(h)], Rc[:, hs(h)], start=True, stop=True), reads=[Pn, Rc], writes=[B[7]])
                c.op("dve", lambda e: e.tensor_tensor(out=Rn[:], in0=B[7][:], in1=Rc[:], op=ALU.add), reads=[B[7], Rc], writes=[Rn])
                Pc, PTc, Rc = Pn, PTn, Rn
            if stage == 6:
                continue
            for h in range(4):
                c.op("pe", lambda e: e.matmul(B[0][:, hs(h)], KBG4[:, hs(h)], TT4[:, hs(h)], start=True, stop=True), reads=[KBG4, TT4], writes=[B[0]])
            acopy(NWT4, NWT4[:], B[0], B[0][:], scale=-1.0)
            for h in range(4):
                c.op("pe", lambda e: e.matmul(B[1][:, hs(h)], TT4[:, hs(h)], VB4[:, hs(h)], start=True, stop=False), reads=[TT4, VB4], writes=[B[1]])
                c.op("pe", lambda e: e.matmul(B[1][:, hs(h)], NWT4[:, hs(h)], Sb4[:, hs(h)], start=False, stop=True), reads=[NWT4, Sb4], writes=[B[1]])
            acopy(VN4, VN4[:], B[1], B[1][:])
            for h in range(4):
                c.op("pe", lambda e: e.matmul(B[2][:, hs(h)], QT[h // 2][:, csl], Sb4[:, hs(h)], start=True, stop=True), reads=[QT[h // 2], Sb4], writes=[B[2]])
            for h in range(4):
                c.op("pe", lambda e: e.matmul(B[3][:, hs(h)], AQT4[:, hs(h)], VN4[:, hs(h)], start=True, stop=True), reads=[AQT4, VN4], writes=[B[3]])
            for h in range(4):
                c.op("pe", lambda e: e.matmul(B[4][:, hs(h)], KG4[:, hs(h)], VN4[:, hs(h)], start=True, stop=True), reads=[KG4, VN4], writes=[B[4]])
            for h in range(4):
                c.op("dve", lambda e: e.scalar_tensor_tensor(out=St4[:, hs(h)], in0=St4[:, hs(h)], scalar=EGL[ch][:, h:h + 1], in1=B[4][:, hs(h)], op0=ALU.mult, op1=ALU.add),
                     reads=[St4, EGL[ch], B[4]], writes=[St4])
            acopy(Sb4, Sb4[:], St4, St4[:])
            acopy(AVs4, AVs4[:], B[3], B[3][:])
            for h in range(4):
                c.op("dve", lambda e: e.scalar_tensor_tensor(out=O4[:, hs(h)], in0=B[2][:, hs(h)], scalar=EGC[ch][:, h:h + 1], in1=AVs4[:, hs(h)], op0=ALU.mult, op1=ALU.add),
                     reads=[B[2], EGC[ch], AVs4], writes=[O4])
            for h in range(4):
                c.op("act", lambda e: e.activation(out=OSQ[:, hs(h)], in_=O4[:, hs(h)], func=AF.Square, accum_out=SSQ[:, h:h + 1]),
                     reads=[O4], writes=[OSQ, SSQ])
            c.op("dve", lambda e: e.tensor_scalar(out=RINV[:], in0=SSQ[:], scalar1=1.0 / 128.0, scalar2=NORM_EPS, op0=ALU.mult, op1=ALU.add), reads=[SSQ], writes=[RINV])
            c.op("act", lambda e: e.activation(out=RINV[:], in_=RINV[:], func=AF.Sqrt), reads=[RINV], writes=[RINV])
            c.op("dve", lambda e: e.reciprocal(out=RINV[:], in_=RINV[:]), reads=[RINV], writes=[RINV])
            out = OUT[ch % 2]
            for h in range(4):
                c.op("dve", lambda e: e.scalar_tensor_tensor(out=OG4[:, hs(h)], in0=O4[:, hs(h)], scalar=RINV[:, h:h + 1], in1=NW[:, hs(h)], op0=ALU.mult, op1=ALU.mult),
                     reads=[O4, RINV, NW], writes=[OG4])
            c.op("pool", lambda e: e.tensor_tensor(out=out[:], in0=OG4[:], in1=SZ[ch][:], op=ALU.mult), reads=[OG4, SZ[ch]], writes=[out])
            c.dma("sp", oo_k[tt * 4 + ch].t, out[:], reads=[out], writes=[oo_k[tt * 4 + ch]])
    c.finish()
    print("GDN program: inst", c.n_inst, "waits", c.n_wait, "dsem", c.ndsem)
    return nc


def prep_G_weights(w_in, conv_w, a_log, dt_bias, norm_w, r):
    qc = [w_in[:, (2 * r + i) * 128:(2 * r + i + 1) * 128] for i in range(2)]
    kc = [w_in[:, 2048 + (2 * r + i) * 128: 2048 + (2 * r + i + 1) * 128] for i in range(2)]
    vc = [w_in[:, 4096 + (4 * r + i) * 128: 4096 + (4 * r + i + 1) * 128] for i in range(4)]
    wfm = np.concatenate(qc + kc + vc, axis=1)
    wf = np.ascontiguousarray(wfm.reshape(16, 128, 1024).transpose(1, 0, 2)).reshape(128, 16 * 1024)
    zc = w_in[:, 8192 + 4 * r * 128: 8192 + (4 * r + 4) * 128]
    bc = w_in[:, 12288 + 4 * r: 12288 + 4 * r + 4]
    ac = w_in[:, 12320 + 4 * r: 12320 + 4 * r + 4]
    wtm = np.concatenate([zc, bc, ac], axis=1)
    wt = np.ascontiguousarray(wtm.reshape(16, 128, 520).transpose(1, 0, 2)).reshape(128, 16 * 520)
    ch_idx = np.concatenate([np.arange((2 * r + i) * 128, (2 * r + i + 1) * 128) for i in range(2)] +
                            [2048 + np.arange((2 * r + i) * 128, (2 * r + i + 1) * 128) for i in range(2)] +
                            [4096 + np.arange((4 * r + i) * 128, (4 * r + i + 1) * 128) for i in range(4)])
    cwm = conv_w[:, ch_idx]
    cw = np.ascontiguousarray(cwm.reshape(4, 8, 128).transpose(2, 1, 0)).reshape(128, 32)
    hp = np.ascontiguousarray(np.broadcast_to(np.concatenate([a_log[4 * r:4 * r + 4], dt_bias[4 * r:4 * r + 4]])[None, :], (128, 8))).astype(np.float32)
    nw = np.ascontiguousarray(np.broadcast_to(np.tile(norm_w, 4)[None, :], (128, 512))).astype(np.float32)
    return {"wf": wf, "wt": wt, "cw": cw, "hp": hp, "nw": nw, "cst": gdn_consts()}


import math

BIGF = 1000.0
TWO_PI = 2.0 * math.pi
C1 = float(np.float32(6.28125))
_rem = TWO_PI - C1
C2 = float(np.float32(np.round(_rem * 2 ** 19) / 2 ** 19))
C3 = float(np.float32(_rem - C2))
PI_SAFE = float(np.nextafter(np.float32(math.pi), np.float32(0)))
QSCALE = 128.0 ** -0.5


def nsa_consts(S):
    NJ = S // 64
    NCMP = S // 16 - 1
    MT = (NCMP + 127) // 128
    perm = np.zeros((128, 128), np.float32)
    for p in range(128):
        perm[(p + 64) % 128, p] = 1.0
    inv = (10000.0 ** (-np.arange(0, 128, 2, dtype=np.float32) / np.float32(128))).astype(np.float32)
    rp = np.zeros((128, 2), np.float32)
    rp[:, 0] = np.tile(inv, 2)
    rp[:64, 1] = -1.0
    rp[64:, 1] = 1.0
    n = np.arange(MT * 128)
    j = np.arange(128)
    ov = np.clip(np.minimum(16 * n[:, None] + 32, 64 * j[None, :] + 64) - np.maximum(16 * n[:, None], 64 * j[None, :]), 0, None).astype(np.float32) / 32.0
    ov[NCMP:, :] = 0.0
    ov[:, NJ:] = 0.0
    ovt = np.ascontiguousarray(ov.reshape(MT, 128, 128).transpose(1, 0, 2)).reshape(128, MT * 128)
    expall = (np.arange(S)[None, :] // 64 == np.arange(128)[:, None]).astype(np.float32)
    return {"perm": perm, "rp": rp, "ov": ovt, "expall": expall}


def nsa_core_consts(S, par):
    NQ = S // 256
    NJ = S // 64
    NCMP = S // 16 - 1
    MT = (NCMP + 127) // 128
    kk = np.arange(128)[:, None]
    qq = np.arange(128)[None, :]
    tri = (kk <= qq).astype(np.float32)
    band = (kk > qq).astype(np.float32)
    one = np.ones((128, 128), np.float32)
    zero = np.zeros((128, 128), np.float32)
    if par == 1:
        maskd = [one, tri]
        wm = [zero, band, one, one, one, tri]
    else:
        maskd = [tri, zero]
        wm = [band, one, one, one, tri, zero]
    cm = np.zeros((NQ, 128, MT * 128), np.float32)
    fq = np.zeros((NQ, 128, 128), np.float32)
    nn = np.arange(MT * 128)
    for i in range(NQ):
        qt = 2 * i + par
        t0 = qt * 128
        tpos = t0 + np.arange(128)
        vis = ((16 * nn[:, None] + 31) <= tpos[None, :]) & (nn[:, None] < NCMP)
        cm[i] = vis.reshape(MT, 128, 128).transpose(1, 0, 2).reshape(128, MT * 128)
        cur = tpos // 64
        jj = np.arange(128)[None, :]
        forced = (jj == 0) | (jj == cur[:, None]) | (jj == cur[:, None] - 1)
        f = np.where(jj <= cur[:, None], np.where(forced, BIGF, 0.0), -BIGF)
        fq[i] = f
    return {"maskd": np.concatenate(maskd, 1), "wm": np.concatenate(wm, 1), "cm": cm, "fq": fq}


def build_N(S=8192):
    NT = S // 512
    NKT = S // 128
    NQ = S // 256
    NCMP = S // 16 - 1
    MT = (NCMP + 127) // 128
    nc = bass.Bass("TRN2", target_bir_lowering=False)
    c = Ctx(nc)
    hT = c.dram("hT", [NT, 128, 16 * 512], BF16, "ExternalInput")
    hq = c.dram("hq", [NQ, 128, 16 * 128], BF16, "ExternalInput")
    pos = c.dram("pos", [NT, 512], I32, "ExternalInput")
    posq = c.dram("posq", [NQ, 128], I32, "ExternalInput")
    wkf = c.dram("wkf", [128, 16 * 512], F32, "ExternalInput")
    wkt = c.dram("wkt", [128, 16 * 256], F32, "ExternalInput")
    wq = c.dram("wq", [128, 16 * 512], F32, "ExternalInput")
    wg = c.dram("wg", [128, 16 * 12], F32, "ExternalInput")
    w1 = c.dram("w1", [2, 128, 32 * 128], F32, "ExternalInput")
    w2 = c.dram("w2", [128, 256], F32, "ExternalInput")
    pet = c.dram("pet", [128, 64], F32, "ExternalInput")
    perm = c.dram("perm", [128, 128], F32, "ExternalInput")
    rp = c.dram("rp", [128, 2], F32, "ExternalInput")
    ov = c.dram("ov", [128, MT * 128], F32, "ExternalInput")
    expall = c.dram("expall", [128, S], F32, "ExternalInput")
    maskd = c.dram("maskd", [128, 256], F32, "ExternalInput")
    wm = c.dram("wm", [128, 768], F32, "ExternalInput")
    cm = c.dram("cm", [NQ, 128, MT * 128], F32, "ExternalInput")
    fq = c.dram("fq", [NQ, 128, 128], F32, "ExternalInput")
    oo = c.dram("oo", [NQ, 128, 512], BF16, "ExternalOutput")
    oo_k = [Buf(oo.t[i], "oo%d" % i) for i in range(NQ)]
    c.dram_out = oo_k

    B = [c.ps("B%d" % i, [128, 512]) for i in range(8)]
    ksT = c.sb("ksT", [128, S], BF16)
    kwT = c.sb("kwT", [128, S], BF16)
    VSW = c.sb("VSW", [128, NKT * 256], BF16)
    KCMP = c.sb("KCMP", [128, MT * 128], BF16)
    VCMP = c.sb("VCMP", [128, MT * 128], BF16)
    PERM = c.sb("PERM", [128, 128], F32)
    RP = c.sb("RP", [128, 2], F32)
    c.dma("sp", PERM[:], perm.t, writes=[PERM])
    c.dma("sp", RP[:], rp.t, writes=[RP])

    def rope_tables(PI_, W, COS, SIN, tmp):
        ANG, KF, R, R2, KI = tmp
        c.op("dve", lambda e: e.tensor_copy(out=ANG[:], in_=PI_[:]), reads=[PI_], writes=[ANG])
        c.op("dve", lambda e: e.tensor_scalar(out=ANG[:], in0=ANG[:], scalar1=RP[:, 0:1], scalar2=None, op0=ALU.mult), reads=[ANG, RP], writes=[ANG])
        c.op("dve", lambda e: e.tensor_scalar(out=KF[:], in0=ANG[:], scalar1=1.0 / TWO_PI, scalar2=None, op0=ALU.mult), reads=[ANG], writes=[KF])
        c.op("dve", lambda e: e.tensor_copy(out=KI[:], in_=KF[:]), reads=[KF], writes=[KI])
        c.op("dve", lambda e: e.tensor_copy(out=KF[:], in_=KI[:]), reads=[KI], writes=[KF])
        c.op("dve", lambda e: e.scalar_tensor_tensor(out=R[:], in0=KF[:], scalar=-C1, in1=ANG[:], op0=ALU.mult, op1=ALU.add), reads=[KF, ANG], writes=[R])
        c.op("dve", lambda e: e.scalar_tensor_tensor(out=R[:], in0=KF[:], scalar=-C2, in1=R[:], op0=ALU.mult, op1=ALU.add), reads=[KF, R], writes=[R])
        c.op("dve", lambda e: e.scalar_tensor_tensor(out=R[:], in0=KF[:], scalar=-C3, in1=R[:], op0=ALU.mult, op1=ALU.add), reads=[KF, R], writes=[R])
        c.op("dve", lambda e: e.tensor_scalar(out=R[:], in0=R[:], scalar1=PI_SAFE, scalar2=-PI_SAFE, op0=ALU.min, op1=ALU.max), reads=[R], writes=[R])
        c.op("act", lambda e: e.activation(out=SIN[:], in_=R[:], func=AF.Sin, scale=RP[:, 1:2]), reads=[R, RP], writes=[SIN])
        c.op("dve", lambda e: e.tensor_scalar(out=R2[:], in0=R[:], scalar1=math.pi / 2, scalar2=None, op0=ALU.add), reads=[R], writes=[R2])
        c.op("dve", lambda e: e.tensor_scalar(out=KF[:], in0=R2[:], scalar1=math.pi, scalar2=-TWO_PI, op0=ALU.is_gt, op1=ALU.mult), reads=[R2], writes=[KF])
        c.op("dve", lambda e: e.tensor_tensor(out=R2[:], in0=R2[:], in1=KF[:], op=ALU.add), reads=[R2, KF], writes=[R2])
        c.op("dve", lambda e: e.tensor_scalar(out=R2[:], in0=R2[:], scalar1=PI_SAFE, scalar2=-PI_SAFE, op0=ALU.min, op1=ALU.max), reads=[R2], writes=[R2])
        c.op("act", lambda e: e.activation(out=COS[:], in_=R2[:], func=AF.Sin), reads=[R2], writes=[COS])

    p1 = []

    def sb1(name, shape, dt):
        g = nc.sbuf_tensor(name, list(shape), dt)
        t = g.__enter__()
        p1.append(g)
        return Buf(t, name)
    Wkf = sb1("Wkf", [128, 16 * 512], BF16)
    Wkt = sb1("Wkt", [128, 16 * 256], BF16)
    W1 = [sb1("W1_%d" % i, [128, 32 * 128], BF16) for i in range(2)]
    W2 = sb1("W2", [128, 256], BF16)
    PET = sb1("PET", [128, 64], BF16)
    kcT = sb1("kcT", [128, S], BF16)
    vcT = sb1("vcT", [128, S], BF16)
    HT = [sb1("HT%d" % i, [128, 16 * 512], BF16) for i in range(2)]
    PB = [sb1("PB%d" % i, [128, 512], I32) for i in range(2)]
    COS = sb1("COS", [128, 512], F32); SIN = sb1("SIN", [128, 512], F32)
    tmpA = [sb1("rt%d" % i, [128, 512], F32) for i in range(4)] + [sb1("rti", [128, 512], I32)]
    Tt = [sb1("Tt%d" % i, [128, 512], F32) for i in range(2)]
    R1 = [sb1("R1_%d" % i, [128, 512], F32) for i in range(2)]
    R2t = [sb1("R2_%d" % i, [128, 512], F32) for i in range(2)]
    C1b = sb1("C1b", [128, 2], F32)
    HSk = sb1("HSk", [128, MT * 128], BF16); HSv = sb1("HSv", [128, MT * 128], BF16)

    c.dma("pool", Wkf[:], wkf.t, writes=[Wkf])
    c.dma("pool", Wkt[:], wkt.t, writes=[Wkt])
    for i in range(2):
        c.dma("pool", W1[i][:], w1.t[i], writes=[W1[i]])
    c.dma("pool", W2[:], w2.t, writes=[W2])
    c.dma("pool", PET[:], pet.t, writes=[PET])

    for tt in range(NT):
        ht = HT[tt % 2]; pb = PB[tt % 2]
        c.dma("sp", ht[:], hT.t[tt], writes=[ht])
        psrc = pos.t[tt]
        pbc = bass.AP(psrc.tensor, psrc.offset, [[0, 128], [1, 512]])
        c.dma("sp", pb[:], pbc, writes=[pb])
        rope_tables(pb, 512, COS, SIN, tmpA)
        tsl = slice(tt * 512, (tt + 1) * 512)
        for cc in range(4):
            pf = B[cc % 2]
            for dc in range(16):
                c.op("pe", lambda e: e.matmul(pf[:], Wkf[:, dc * 512 + cc * 128: dc * 512 + (cc + 1) * 128], ht[:, dc * 512:(dc + 1) * 512],
                                              start=(dc == 0), stop=(dc == 15)), reads=[Wkf, ht], writes=[pf])
            if cc == 3:
                c.op("act", lambda e: e.activation(out=vcT[:, tsl], in_=pf[:], func=AF.Copy), reads=[pf], writes=[vcT])
                continue
            dst = (kcT, ksT, kwT)[cc]
            T = Tt[cc % 2]
            c.op("act", lambda e: e.activation(out=T[:], in_=pf[:], func=AF.Copy), reads=[pf], writes=[T])
            sw = B[2 + cc % 2]
            c.op("pe", lambda e: e.matmul(sw[:], PERM[:], T[:], start=True, stop=True), reads=[PERM, T], writes=[sw])
            r1 = R1[cc % 2]; r2 = R2t[cc % 2]
            c.op("pool", lambda e: e.tensor_tensor(out=r1[:], in0=T[:], in1=COS[:], op=ALU.mult), reads=[T, COS], writes=[r1])
            c.op("dve", lambda e: e.tensor_tensor(out=r2[:], in0=sw[:], in1=SIN[:], op=ALU.mult), reads=[sw, SIN], writes=[r2])
            c.op("pool", lambda e: e.tensor_tensor(out=dst[:, tsl], in0=r1[:], in1=r2[:], op=ALU.add), reads=[r1, r2], writes=[dst])
        for ch in range(4):
            pz = B[4 + ch % 2]
            for dc in range(16):
                c.op("pe", lambda e: e.matmul(pz[:, 0:256], ht[:, dc * 512 + ch * 128: dc * 512 + (ch + 1) * 128], Wkt[:, dc * 256:(dc + 1) * 256],
                                              start=(dc == 0), stop=(dc == 15)), reads=[ht, Wkt], writes=[pz])
            kt = tt * 4 + ch
            c.op("act", lambda e: e.activation(out=VSW[:, kt * 256:(kt + 1) * 256], in_=pz[:, 0:256], func=AF.Copy), reads=[pz], writes=[VSW])

    for kv in range(2):
        src = kcT if kv == 0 else vcT
        HS = HSk if kv == 0 else HSv
        for p in range(32):
            c.op("pe", lambda e: e.matmul(B[6][:, 0:1], W1[kv][:, p * 128:(p + 1) * 128], PET[:, kv * 32 + p: kv * 32 + p + 1], start=(p == 0), stop=(p == 31)),
                 reads=[W1[kv], PET], writes=[B[6]])
        c.op("act", lambda e: e.activation(out=C1b[:, kv:kv + 1], in_=B[6][:, 0:1], func=AF.Copy), reads=[B[6]], writes=[C1b])
        c.op("dve", lambda e: e.memset(HS[:], 0.0), writes=[HS])
        for m in range(MT):
            n0 = m * 128
            nn = min(128, NCMP - n0)
            for p in range(32):
                st_ = p + 16 * n0
                c.op("pe", lambda e: e.matmul(B[7][:, 0:nn], W1[kv][:, p * 128:(p + 1) * 128], src[:, st_: st_ + 16 * (nn - 1) + 1: 16], start=(p == 0), stop=(p == 31)),
                     reads=[W1[kv], src], writes=[B[7]])
            c.op("act", lambda e: e.activation(out=HS[:, n0:n0 + nn], in_=B[7][:, 0:nn], func=AF.Silu, bias=C1b[:, kv:kv + 1]), reads=[B[7], C1b], writes=[HS])
        if kv == 0:
            for m in range(MT):
                c.op("pe", lambda e: e.matmul(B[6][:, m * 128:(m + 1) * 128], W2[:, 0:128], HS[:, m * 128:(m + 1) * 128], start=True, stop=True), reads=[W2, HS], writes=[B[6]])
            c.op("act", lambda e: e.activation(out=KCMP[:], in_=B[6][:, 0:MT * 128], func=AF.Copy), reads=[B[6]], writes=[KCMP])
        else:
            for m in range(MT):
                c.op("pe", lambda e: e.matmul(B[6][:, m * 128:(m + 1) * 128], HS[:, m * 128:(m + 1) * 128], W2[:, 128:256], start=True, stop=True), reads=[W2, HS], writes=[B[6]])
            c.op("act", lambda e: e.activation(out=VCMP[:], in_=B[6][:, 0:MT * 128], func=AF.Copy), reads=[B[6]], writes=[VCMP])

    c.barrier()
    for g_ in reversed(p1):
        g_.__exit__(None, None, None)

    EXPALL = c.sb("EXPALL", [128, S], BF16)
    Wq = c.sb("Wq", [128, 16 * 512], BF16)
    Wg = c.sb("Wg", [128, 16 * 12], BF16)
    OV = c.sb("OV", [128, MT * 128], BF16)
    MASKD = c.sb("MASKD", [128, 256], BF16)
    WM = c.sb("WM", [128, 768], BF16)
    ONEc = c.sb("ONEc", [128, 1], BF16)
    IDb = c.sb("IDb", [128, 128], BF16)
    HQ = [c.sb("HQ%d" % i, [128, 16 * 128], BF16) for i in range(2)]
    PQ = [c.sb("PQ%d" % i, [128, 128], I32) for i in range(2)]
    CMt = [c.sb("CMt%d" % i, [128, MT * 128], BF16) for i in range(2)]
    FQt = [c.sb("FQt%d" % i, [128, 128], F32) for i in range(2)]
    COSq = c.sb("COSq", [128, 128], F32); SINq = c.sb("SINq", [128, 128], F32)
    tmpQ = [c.sb("rq%d" % i, [128, 128], F32) for i in range(4)] + [c.sb("rqi", [128, 128], I32)]
    COS4 = c.sb("COS4", [128, 512], F32); SIN4 = c.sb("SIN4", [128, 512], F32)
    TQ = c.sb("TQ", [128, 512], F32); RQ1 = c.sb("RQ1", [128, 512], F32); RQ2 = c.sb("RQ2", [128, 512], F32)
    qT = c.sb("qT", [128, 512], BF16)
    GATE = c.sb("GATE", [128, 12], F32)
    PC = [c.sb("PC%d" % m, [128, 512], BF16) for m in range(MT)]
    EE = [c.sb("EE%d" % i, [128, 512], BF16) for i in range(2)]
    PP = [c.sb("PP%d" % i, [128, 512], BF16) for i in range(2)]
    MXs = [c.sb("MXs%d" % i, [128, 128], BF16) for i in range(2)]
    OCs = c.sb("OCs", [128, 512], F32)
    RD = c.sb("RD", [128, 12], F32); CF = c.sb("CF", [128, 12], F32)
    IMPF = c.sb("IMPF", [128, 128], F32); WKt = c.sb("WKt", [128, 128], F32)
    M8 = c.sb("M8", [128, 16], F32)
    SEL = c.sb("SEL", [128, 128], BF16); SELT = c.sb("SELT", [128, 128], BF16)
    OACC = c.sb("OACC", [128, 512], F32)
    OUTq = [c.sb("OUTq%d" % i, [128, 512], BF16) for i in range(2)]

    c.dma("pool", EXPALL[:], expall.t, writes=[EXPALL])
    c.dma("pool", Wq[:], wq.t, writes=[Wq])
    c.dma("pool", Wg[:], wg.t, writes=[Wg])
    c.dma("pool", OV[:], ov.t, writes=[OV])
    c.dma("pool", MASKD[:], maskd.t, writes=[MASKD])
    c.dma("pool", WM[:], wm.t, writes=[WM])
    c.op("dve", lambda e: e.memset(ONEc[:], 1.0), writes=[ONEc])
    c.op("pe", lambda e: e.matmul(B[0][:, 0:128], PERM[:], PERM[:], start=True, stop=True), reads=[PERM], writes=[B[0]])
    c.op("act", lambda e: e.activation(out=IDb[:], in_=B[0][:, 0:128], func=AF.Copy), reads=[B[0]], writes=[IDb])

    def hs(g):
        return slice(g * 128, (g + 1) * 128)

    def bc4(ap):
        return bass.AP(ap.tensor, ap.offset, [list(ap.ap[0]), [0, 4], [1, 128]])

    def v3(ap):
        return ap.rearrange("p (g q) -> p g q", g=4)

    def pv_den(P, kt, vofs, obank, dcol, first, last):
        for g in range(4):
            c.op("pe", lambda e: e.matmul(obank[:, hs(g)], P[:, hs(g)], VSW[:, kt * 256 + vofs: kt * 256 + vofs + 128],
                                          start=(first and g == 0), stop=last, skip_group_check=True), reads=[P, VSW], writes=[obank])
        for g in range(4):
            c.op("pe", lambda e: e.matmul(B[7][:, dcol + g: dcol + g + 1], P[:, hs(g)], ONEc[:], start=(first and g == 0), stop=last, skip_group_check=True),
                 reads=[P, ONEc], writes=[B[7]])

    for i in range(NQ):
        hqb = HQ[i % 2]; pq = PQ[i % 2]; cmt = CMt[i % 2]; fqt = FQt[i % 2]
        c.dma("sp", hqb[:], hq.t[i], writes=[hqb])
        ps_ = posq.t[i]
        c.dma("sp", pq[:], bass.AP(ps_.tensor, ps_.offset, [[0, 128], [1, 128]]), writes=[pq])
        c.dma("pool", cmt[:], cm.t[i], writes=[cmt])
        c.dma("sp", fqt[:], fq.t[i], writes=[fqt])
        rope_tables(pq, 128, COSq, SINq, tmpQ)
        for g in range(4):
            c.op("pool", lambda e: e.tensor_copy(out=COS4[:, hs(g)], in_=COSq[:]), reads=[COSq], writes=[COS4])
            c.op("pool", lambda e: e.tensor_copy(out=SIN4[:, hs(g)], in_=SINq[:]), reads=[SINq], writes=[SIN4])
        for g in range(4):
            for dc in range(16):
                c.op("pe", lambda e: e.matmul(B[0][:, hs(g)], Wq[:, dc * 512 + g * 128: dc * 512 + (g + 1) * 128], hqb[:, dc * 128:(dc + 1) * 128],
                                              start=(dc == 0), stop=(dc == 15)), reads=[Wq, hqb], writes=[B[0]])
        for dc in range(16):
            c.op("pe", lambda e: e.matmul(B[4][:, 0:12], hqb[:, dc * 128:(dc + 1) * 128], Wg[:, dc * 12:(dc + 1) * 12], start=(dc == 0), stop=(dc == 15)),
                 reads=[hqb, Wg], writes=[B[4]])
        c.op("act", lambda e: e.activation(out=TQ[:], in_=B[0][:], func=AF.Copy), reads=[B[0]], writes=[TQ])
        c.op("act", lambda e: e.activation(out=GATE[:], in_=B[4][:, 0:12], func=AF.Sigmoid), reads=[B[4]], writes=[GATE])
        c.op("pe", lambda e: e.matmul(B[1][:], PERM[:], TQ[:], start=True, stop=True), reads=[PERM, TQ], writes=[B[1]])
        c.op("pool", lambda e: e.tensor_tensor(out=RQ1[:], in0=TQ[:], in1=COS4[:], op=ALU.mult), reads=[TQ, COS4], writes=[RQ1])
        c.op("dve", lambda e: e.tensor_tensor(out=RQ2[:], in0=B[1][:], in1=SIN4[:], op=ALU.mult), reads=[B[1], SIN4], writes=[RQ2])
        c.op("dve", lambda e: e.scalar_tensor_tensor(out=qT[:], in0=RQ1[:], scalar=1.0, in1=RQ2[:], op0=ALU.mult, op1=ALU.add), reads=[RQ1, RQ2], writes=[qT])
        c.op("dve", lambda e: e.tensor_scalar(out=qT[:], in0=qT[:], scalar1=QSCALE, scalar2=None, op0=ALU.mult), reads=[qT], writes=[qT])
        for m in range(MT):
            sb_ = B[2 + m % 2]
            c.op("pe", lambda e: e.matmul(sb_[:], KCMP[:, hs(m)], qT[:], start=True, stop=True), reads=[KCMP, qT], writes=[sb_])
            c.op("act", lambda e: e.activation(out=PC[m][:], in_=sb_[:], func=AF.Exp), reads=[sb_], writes=[PC[m]])
            c.op("dve", lambda e: e.tensor_tensor(out=v3(PC[m][:]), in0=v3(PC[m][:]), in1=bc4(cmt[:, hs(m)]), op=ALU.mult), reads=[PC[m], cmt], writes=[PC[m]])
        for g in range(4):
            for m in range(MT):
                c.op("pe", lambda e: e.matmul(B[5][:, hs(g)], PC[m][:, hs(g)], VCMP[:, hs(m)], start=(m == 0), stop=(m == MT - 1)), reads=[PC[m], VCMP], writes=[B[5]])
        for g in range(4):
            for m in range(MT):
                c.op("pe", lambda e: e.matmul(B[7][:, g:g + 1], PC[m][:, hs(g)], ONEc[:], start=(m == 0), stop=(m == MT - 1)), reads=[PC[m], ONEc], writes=[B[7]])
        for g in range(4):
            for m in range(MT):
                c.op("pe", lambda e: e.matmul(B[1][:, hs(g)], PC[m][:, hs(g)], OV[:, hs(m)], start=(m == 0), stop=(m == MT - 1)), reads=[PC[m], OV], writes=[B[1]])
        c.op("act", lambda e: e.activation(out=OCs[:], in_=B[5][:], func=AF.Copy), reads=[B[5]], writes=[OCs])
        c.op("dve", lambda e: e.tensor_scalar(out=RD[:, 0:4], in0=B[7][:, 0:4], scalar1=1e-30, scalar2=None, op0=ALU.max), reads=[B[7]], writes=[RD])
        c.op("dve", lambda e: e.reciprocal(out=RD[:, 0:4], in_=RD[:, 0:4]), reads=[RD], writes=[RD])
        c.op("dve", lambda e: e.scalar_tensor_tensor(out=IMPF[:], in0=B[1][:, hs(0)], scalar=RD[:, 0:1], in1=fqt[:], op0=ALU.mult, op1=ALU.add), reads=[B[1], RD, fqt], writes=[IMPF])
        for g in range(1, 4):
            c.op("dve", lambda e: e.scalar_tensor_tensor(out=IMPF[:], in0=B[1][:, hs(g)], scalar=RD[:, g:g + 1], in1=IMPF[:], op0=ALU.mult, op1=ALU.add), reads=[B[1], RD, IMPF], writes=[IMPF])
        c.op("dve", lambda e: e.max(out=M8[:, 0:8], in_=IMPF[:]), reads=[IMPF], writes=[M8])
        c.op("dve", lambda e: e.match_replace(out=WKt[:], in_to_replace=M8[:, 0:8], in_values=IMPF[:], imm_value=-1e30), reads=[IMPF, M8], writes=[WKt])
        c.op("dve", lambda e: e.max(out=M8[:, 8:16], in_=WKt[:]), reads=[WKt], writes=[M8])
        c.op("dve", lambda e: e.tensor_scalar(out=SEL[:], in0=IMPF[:], scalar1=M8[:, 15:16], scalar2=None, op0=ALU.is_ge), reads=[IMPF, M8], writes=[SEL])
        TPv = B[1].t[:, :].bitcast(BF16)
        c.op("pe", lambda e: e.transpose(TPv[:, 0:128], SEL[:], IDb[:]), reads=[SEL, IDb], writes=[B[1]])
        c.op("act", lambda e: e.activation(out=SELT[:], in_=TPv[:, 0:128], func=AF.Copy), reads=[B[1]], writes=[SELT])
        nkt = 2 * i + 2
        for kt in range(nkt):
            sb_ = B[2 + kt % 2]; mx = B[kt % 2]
            c.op("pe", lambda e: e.matmul(sb_[:], ksT[:, hs(kt)], qT[:], start=True, stop=True), reads=[ksT, qT], writes=[sb_])
            c.op("pe", lambda e: e.matmul(mx[:, 0:128], EXPALL[:, hs(kt)], SELT[:], start=True, stop=True), reads=[EXPALL, SELT], writes=[mx])
            ee = EE[kt % 2]; pp = PP[kt % 2]
            c.op("act", lambda e: e.activation(out=ee[:], in_=sb_[:], func=AF.Exp), reads=[sb_], writes=[ee])
            if kt >= nkt - 2:
                w = kt - (nkt - 2)
                ms_ = MXs[w]
                c.op("dve", lambda e: e.tensor_tensor(out=ms_[:], in0=mx[:, 0:128], in1=MASKD[:, hs(w)], op=ALU.mult), reads=[mx, MASKD], writes=[ms_])
                c.op("dve", lambda e: e.tensor_tensor(out=v3(pp[:]), in0=v3(ee[:]), in1=bc4(ms_[:, :]), op=ALU.mult), reads=[ee, ms_], writes=[pp])
            else:
                c.op("dve", lambda e: e.tensor_tensor(out=v3(pp[:]), in0=v3(ee[:]), in1=bc4(mx[:, 0:128]), op=ALU.mult), reads=[ee, mx], writes=[pp])
            pv_den(pp, kt, 0, B[6], 4, kt == 0, kt == nkt - 1)
        wl = [w for w in range(6) if 2 * i - 4 + w >= 0]
        for idx, w in enumerate(wl):
            kt = 2 * i - 4 + w
            sb_ = B[2 + idx % 2]
            c.op("pe", lambda e: e.matmul(sb_[:], kwT[:, hs(kt)], qT[:], start=True, stop=True), reads=[kwT, qT], writes=[sb_])
            ee = EE[idx % 2]; pp = PP[idx % 2]
            c.op("act", lambda e: e.activation(out=ee[:], in_=sb_[:], func=AF.Exp), reads=[sb_], writes=[ee])
            c.op("dve", lambda e: e.tensor_tensor(out=v3(pp[:]), in0=v3(ee[:]), in1=bc4(WM[:, hs(w)]), op=ALU.mult), reads=[ee, WM], writes=[pp])
            pv_den(pp, kt, 128, B[5], 8, idx == 0, idx == len(wl) - 1)
        c.op("dve", lambda e: e.tensor_scalar(out=RD[:], in0=B[7][:, 0:12], scalar1=1e-30, scalar2=None, op0=ALU.max), reads=[B[7]], writes=[RD])
        c.op("dve", lambda e: e.reciprocal(out=RD[:], in_=RD[:]), reads=[RD], writes=[RD])
        c.op("dve", lambda e: e.tensor_tensor(out=CF[:].rearrange("p (b g) -> p b g", b=3), in0=RD[:].rearrange("p (b g) -> p b g", b=3),
                                              in1=GATE[:].rearrange("p (g b) -> p b g", b=3), op=ALU.mult), reads=[RD, GATE], writes=[CF])
        outb = OUTq[i % 2]
        for g in range(4):
            c.op("dve", lambda e: e.tensor_scalar(out=OACC[:, hs(g)], in0=OCs[:, hs(g)], scalar1=CF[:, g:g + 1], scalar2=None, op0=ALU.mult), reads=[OCs, CF], writes=[OACC])
            c.op("dve", lambda e: e.scalar_tensor_tensor(out=OACC[:, hs(g)], in0=B[6][:, hs(g)], scalar=CF[:, 4 + g:5 + g], in1=OACC[:, hs(g)], op0=ALU.mult, op1=ALU.add),
                 reads=[B[6], CF, OACC], writes=[OACC])
            c.op("dve", lambda e: e.scalar_tensor_tensor(out=outb[:, hs(g)], in0=B[5][:, hs(g)], scalar=CF[:, 8 + g:9 + g], in1=OACC[:, hs(g)], op0=ALU.mult, op1=ALU.add),
                 reads=[B[5], CF, OACC], writes=[outb])
        c.dma("sp", oo_k[i].t, outb[:], reads=[outb], writes=[oo_k[i]])
    c.finish()
    print("NSA program: inst", c.n_inst, "waits", c.n_wait, "dsem", c.ndsem)
    return nc


def prep_N_weights(w_in, cmp_pe, cmp_w1, cmp_w2, r):
    hk = r // 2
    DQ = 2048; DKV = 512

    def fmaj(m):
        C = m.shape[1]
        return np.ascontiguousarray(m.reshape(16, 128, C).transpose(1, 0, 2)).reshape(128, 16 * C)
    col = lambda i: w_in[:, DQ + i * DKV + hk * 128: DQ + i * DKV + (hk + 1) * 128]
    wkf = fmaj(np.concatenate([col(0), col(2), col(4), col(1)], 1))
    wkt = fmaj(np.concatenate([col(3), col(5)], 1))
    wq = fmaj(w_in[:, hk * 512:(hk + 1) * 512])
    g0 = DQ + 6 * DKV + hk * 12
    wg = fmaj(w_in[:, g0:g0 + 12])
    w1 = np.stack([np.ascontiguousarray(cmp_w1[kv].reshape(32, 128, 128).transpose(1, 0, 2)).reshape(128, 32 * 128) for kv in range(2)])
    w2 = np.concatenate([cmp_w2[0], cmp_w2[1]], 1)
    pet = np.concatenate([cmp_pe[0].T, cmp_pe[1].T], 1)
    return {"wkf": wkf, "wkt": wkt, "wq": wq, "wg": wg, "w1": w1, "w2": np.ascontiguousarray(w2), "pet": np.ascontiguousarray(pet)}


NCH_M = 80


def build_M():
    nc = bass.Bass("TRN2", target_bir_lowering=False)
    c = Ctx(nc)
    cT = c.dram("cT", [128, 16], F32, "ExternalInput")
    wm = c.dram("wm", [NCH_M, 128, 16 * 128], F32, "ExternalInput")
    bm = c.dram("bm", [128, NCH_M], F32, "ExternalInput")
    xT = c.dram("xT", [16, 128, 1024], F32, "ExternalInput")
    mo = c.dram("mo", [128, NCH_M], F32, "ExternalOutput")
    h0 = c.dram("h0", [16, 128, 1024], BF16, "ExternalOutput")
    h0_k = [Buf(h0.t[i], "h0_%d" % i) for i in range(16)]
    c.dram_out = [mo] + h0_k
    CT = c.sb("CT", [128, 16], F32)
    COND = c.sb("COND", [128, 16], F32)
    BM = c.sb("BM", [128, NCH_M], F32)
    MO = c.sb("MO", [128, NCH_M], F32)
    SC1 = c.sb("SC1", [128, 16], F32)
    WB = [c.sb("WB%d" % i, [128, 16 * 128], F32) for i in range(3)]
    XB = [c.sb("XB%d" % i, [128, 1024], F32) for i in range(2)]
    HB = [c.sb("HB%d" % i, [128, 1024], BF16) for i in range(2)]
    P = [c.ps("P%d" % i, [128, 512]) for i in range(2)]
    c.dma("sp", CT[:], cT.t, writes=[CT])
    c.dma("sp", BM[:], bm.t, writes=[BM])
    c.op("act", lambda e: e.activation(out=COND[:], in_=CT[:], func=AF.Silu), reads=[CT], writes=[COND])
    for ch in range(NCH_M):
        wb = WB[ch % 3]
        c.dma("sp" if ch % 2 == 0 else "pool", wb[:], wm.t[ch], writes=[wb])
        pp = P[(ch // 8) % 2]
        col = ch % 8
        for dc in range(16):
            c.op("pe", lambda e: e.matmul(pp[:, col:col + 1], wb[:, dc * 128:(dc + 1) * 128], COND[:, dc:dc + 1], start=(dc == 0), stop=(dc == 15)),
                 reads=[wb, COND], writes=[pp])
        if col == 7:
            g0 = ch - 7
            c.op("dve", lambda e: e.tensor_tensor(out=MO[:, g0:g0 + 8], in0=pp[:, 0:8], in1=BM[:, g0:g0 + 8], op=ALU.add), reads=[pp, BM], writes=[MO])
    c.dma("sp", mo.t, MO[:], reads=[MO], writes=[mo])
    c.op("dve", lambda e: e.tensor_scalar(out=SC1[:], in0=MO[:, 64:80], scalar1=1.0, scalar2=None, op0=ALU.add), reads=[MO], writes=[SC1])
    for dc in range(16):
        xb = XB[dc % 2]; hb = HB[dc % 2]
        c.dma("sp", xb[:], xT.t[dc], writes=[xb])
        c.op("act", lambda e: e.activation(out=hb[:], in_=xb[:], func=AF.Identity, scale=SC1[:, dc:dc + 1], bias=MO[:, 48 + dc:49 + dc]), reads=[xb, SC1, MO], writes=[hb])
        c.dma("act", h0_k[dc].t, hb[:], reads=[hb], writes=[h0_k[dc]])
    c.finish()
    return nc


def prep_M(c, mod_w, mod_b, r):
    ids = [(g // 96, g % 96) for g in range(r * 48, (r + 1) * 48)] + [(0, j) for j in range(32)]
    wm = np.empty((NCH_M, 128, 16 * 128), np.float32)
    bm = np.empty((128, NCH_M), np.float32)
    for n, (L, j) in enumerate(ids):
        blk = mod_w[L][:, j * 128:(j + 1) * 128]
        wm[n] = blk.reshape(16, 128, 128).transpose(1, 0, 2).reshape(128, 16 * 128)
        bm[:, n] = mod_b[L][j * 128:(j + 1) * 128]
    cT = np.ascontiguousarray(c.reshape(16, 128).T)
    return {"cT": cT, "wm": wm, "bm": bm}


_NC_CACHE = {}
_DBG = {}
_DBG_ON = False


def _prog(key, fn):
    if key not in _NC_CACHE:
        _NC_CACHE[key] = fn()
    return _NC_CACHE[key]


def _run(nc, ims):
    res = run_bass_kernel_spmd(nc, ims, core_ids=list(range(8)))
    return res.results


def kernel(x, c, positions, mod_w, mod_b, ln_g, ln_b, ffn_w_gu, ffn_w_down,
           gdn_w_in, gdn_conv_w, gdn_a_log, gdn_dt_bias, gdn_norm_w, gdn_w_out,
           nsa_w_in, nsa_cmp_pe, nsa_cmp_w1, nsa_cmp_w2, nsa_w_out):
    f32 = lambda a: np.asarray(a, dtype=np.float32)
    x = f32(x); c = f32(c); mod_w = f32(mod_w); mod_b = f32(mod_b); ln_g = f32(ln_g); ln_b = f32(ln_b)
    ffn_w_gu = f32(ffn_w_gu); ffn_w_down = f32(ffn_w_down)
    gdn_w_in = f32(gdn_w_in); gdn_conv_w = f32(gdn_conv_w); gdn_a_log = f32(gdn_a_log); gdn_dt_bias = f32(gdn_dt_bias)
    gdn_norm_w = f32(gdn_norm_w); gdn_w_out = f32(gdn_w_out)
    nsa_w_in = f32(nsa_w_in); nsa_cmp_pe = f32(nsa_cmp_pe); nsa_cmp_w1 = f32(nsa_cmp_w1); nsa_cmp_w2 = f32(nsa_cmp_w2); nsa_w_out = f32(nsa_w_out)
    positions = np.asarray(positions, dtype=np.int32)[0]
    S = 8192
    NTT = S // 512
    xT_full = np.ascontiguousarray(x[0].T).reshape(16, 128, S)
    xT_cores = [np.ascontiguousarray(xT_full[:, :, r * 1024:(r + 1) * 1024]) for r in range(8)]

    ncM = _prog("M", build_M)
    ims = []
    for r in range(8):
        im = prep_M(c[0], mod_w, mod_b, r)
        im["xT"] = xT_cores[r]
        ims.append(im)
    res = _run(ncM, ims)
    mods = np.empty((4 * 96, 128), np.float32)
    for r in range(8):
        mo = res[r]["mo"]
        mods[r * 48:(r + 1) * 48, :] = mo[:, 0:48].T
    mods = mods.reshape(4, 12288)
    if _DBG_ON:
        _DBG["mods"] = mods.copy()
    hT_full = np.concatenate([res[r]["h0"] for r in range(8)], axis=2)

    gdn_nc = None
    for i in range(4):
        j = i // 2
        hT_tiles = np.ascontiguousarray(hT_full.reshape(16, 128, NTT, 512).transpose(2, 1, 0, 3)).reshape(NTT, 128, 16 * 512)
        if i % 2 == 0:
            ncA = _prog("G", lambda: build_G(NTT))
            ims = []
            for r in range(8):
                im = prep_G_weights(gdn_w_in[j], gdn_conv_w[j], gdn_a_log[j], gdn_dt_bias[j], gdn_norm_w[j], r)
                im["hT"] = hT_tiles
                ims.append(im)
            res = _run(ncA, ims)
            O_full = np.concatenate([res[r]["oo"].reshape(S, 512) for r in range(8)], axis=1)
            w_out = gdn_w_out[j]
        else:
            ncA = _prog("N", lambda: build_N(S))
            cst = nsa_consts(S)
            cc = [nsa_core_consts(S, 0), nsa_core_consts(S, 1)]
            hq_all = hT_full.reshape(16, 128, S // 128, 128)
            pos_t = np.ascontiguousarray(positions.reshape(NTT, 512))
            pos_q = positions.reshape(S // 128, 128)
            ims = []
            for r in range(8):
                par = r % 2
                qts = [2 * q + par for q in range(S // 256)]
                im = prep_N_weights(nsa_w_in[j], nsa_cmp_pe[j], nsa_cmp_w1[j], nsa_cmp_w2[j], r)
                im.update(cst)
                im.update(cc[par])
                im["hT"] = hT_tiles
                im["hq"] = np.ascontiguousarray(hq_all[:, :, qts, :].transpose(2, 1, 0, 3)).reshape(S // 256, 128, 16 * 128)
                im["pos"] = pos_t
                im["posq"] = np.ascontiguousarray(pos_q[qts])
                ims.append(im)
            res = _run(ncA, ims)
            O_full = np.empty((S // 128, 128, 2048), dtype=NPBF)
            for r in range(8):
                hk = r // 2; par = r % 2
                O_full[par::2, :, hk * 512:(hk + 1) * 512] = res[r]["oo"]
            O_full = O_full.reshape(S, 2048)
            w_out = nsa_w_out[j]
        KC = O_full.shape[1] // 128
        ncB = _prog("B%d" % KC, lambda: build_B(KC))
        oT = np.ascontiguousarray(O_full.T).reshape(KC, 128, S)
        wo, wgu, wd = prep_B_weights(w_out, ffn_w_gu[i], ffn_w_down[i])
        mi = mods[i]
        mn = mods[i + 1] if i + 1 < 4 else np.zeros(12288, np.float32)
        modv = np.ascontiguousarray(np.concatenate([fm(v) for v in np.split(mi, 6)] + [fm(mn[0:2048]), fm(mn[2048:4096])], axis=1))
        lnp = np.ascontiguousarray(np.concatenate([fm(ln_g[i, 0]), fm(ln_b[i, 0]), fm(ln_g[i, 1]), fm(ln_b[i, 1])], axis=1))
        ims = []
        for r in range(8):
            oTr = np.ascontiguousarray(oT[:, :, r * 1024:(r + 1) * 1024].transpose(1, 0, 2)).reshape(128, KC * 1024)
            ims.append({"oT": oTr, "xT": xT_cores[r], "modv": modv, "lnp": lnp, "wo": wo, "wgu": wgu, "wd": wd})
        res = _run(ncB, ims)
        xT_cores = [res[r]["xo"] for r in range(8)]
        hT_full = np.concatenate([res[r]["hn"] for r in range(8)], axis=2)
        if _DBG_ON:
            _DBG["O%d" % i] = O_full
            _DBG["x%d" % i] = np.concatenate(xT_cores, axis=2).reshape(2048, S).T.copy()
    xT_out = np.concatenate(xT_cores, axis=2).reshape(2048, S)
    return np.ascontiguousarray(xT_out.T).reshape(1, S, 2048).astype(np.float32)
```
